# Optimizing a Trainium2 kernel written in Bass

```python
import jax, jax.numpy as jnp
from jax import lax
import numpy as np

D_MODEL = 1024
BATCH = 16
SEQ = 2048
DEPTH = 2

MLA_HEADS = 8
QK_NOPE = 64
QK_ROPE = 32
V_HEAD = 64
Q_LORA = 256
KV_LORA = 256
ROPE_THETA = 10000.0
ATTN_BLOCK = 128
SC_GROUPS = 8
SC_WIDTH = 512
SC_K = 3
CF_GROUPS = 8
CF_WIDTH = 512
CF_K = 31
GM_GROUPS = 4
GM_WIDTH = 512
GM_CHUNK = 128
N_BRANCH = 4
FFN_HIDDEN = -(-8 * D_MODEL // (3 * 256)) * 256
N_IN = Q_LORA + KV_LORA + QK_ROPE + 3 * SC_WIDTH + 2 * CF_WIDTH + 2 * GM_WIDTH + N_BRANCH * D_MODEL
EPS = 1e-6

kernel_name = "hybrid_gated_parallel_encoder"


def rms_norm(x, g):
    x32 = x.astype(jnp.float32)
    y = x32 * lax.rsqrt(jnp.mean(x32 * x32, axis=-1, keepdims=True) + EPS)
    return (y * g.astype(jnp.float32)).astype(x.dtype)


def layer_norm(x, g, b):
    x32 = x.astype(jnp.float32)
    mu = jnp.mean(x32, axis=-1, keepdims=True)
    xc = x32 - mu
    var = jnp.mean(xc * xc, axis=-1, keepdims=True)
    y = xc * lax.rsqrt(var + EPS) * g.astype(jnp.float32) + b.astype(jnp.float32)
    return y.astype(x.dtype)


def apply_rope(t, cos, sin):
    cos = cos.astype(t.dtype)
    sin = sin.astype(t.dtype)
    t1, t2 = jnp.split(t, 2, axis=-1)
    return jnp.concatenate([t1 * cos - t2 * sin, t2 * cos + t1 * sin], axis=-1)


def split_points():
    sizes = [Q_LORA, KV_LORA, QK_ROPE, SC_WIDTH, SC_WIDTH, SC_WIDTH,
             CF_WIDTH, CF_WIDTH, GM_WIDTH, GM_WIDTH]
    pts, acc = [], 0
    for s in sizes:
        acc += s
        pts.append(acc)
    return pts


def mla_branch(c_q, c_kv, k_rope, cos, sin, q_norm, w_uq, kv_norm, w_ukv):
    B, S, _ = c_q.shape
    q = (rms_norm(c_q, q_norm) @ w_uq).reshape(B, S, MLA_HEADS, QK_NOPE + QK_ROPE)
    q_nope = q[..., :QK_NOPE]
    q_rope = apply_rope(q[..., QK_NOPE:], cos[:, :, None, :], sin[:, :, None, :])
    kv = (rms_norm(c_kv, kv_norm) @ w_ukv).reshape(B, S, MLA_HEADS, QK_NOPE + V_HEAD)
    k_nope = kv[..., :QK_NOPE]
    v = kv[..., QK_NOPE:]
    k_rope = apply_rope(k_rope, cos, sin)
    scale = (QK_NOPE + QK_ROPE) ** -0.5
    nb = S // ATTN_BLOCK

    def blocks(t):
        return jnp.moveaxis(t.reshape(B, nb, ATTN_BLOCK, *t.shape[2:]), 1, 0)

    def attend(qs):
        qn, qr = qs
        s = (jnp.einsum('bqhd,bkhd->bhqk', qn, k_nope)
             + jnp.einsum('bqhr,bkr->bhqk', qr, k_rope))
        p = jax.nn.softmax(s.astype(jnp.float32) * scale, axis=-1).astype(v.dtype)
        return jnp.einsum('bhqk,bkhd->bqhd', p, v)

    o = lax.map(attend, (blocks(q_nope), blocks(q_rope)))
    return jnp.moveaxis(o, 0, 1).reshape(B, S, MLA_HEADS * V_HEAD)


def short_conv_branch(b_gate, c_gate, x_in, w):
    S = x_in.shape[1]
    z = c_gate * x_in
    pad = SC_K // 2
    zp = jnp.pad(z, ((0, 0), (pad, pad), (0, 0)))
    w = w.astype(z.dtype)
    y = zp[:, 0:S] * w[0]
    for k in range(1, SC_K):
        y = y + zp[:, k:k + S] * w[k]
    return b_gate * y


def conformer_branch(a, gate, conv_w, conv_b, ln_g, ln_b):
    z = a * jax.nn.sigmoid(gate)
    y = lax.conv_general_dilated(
        z, conv_w[:, None, :].astype(z.dtype), window_strides=(1,),
        padding=[(CF_K // 2, CF_K // 2)], dimension_numbers=('NWC', 'WIO', 'NWC'),
        feature_group_count=CF_WIDTH) + conv_b
    return jax.nn.silu(layer_norm(y, ln_g, ln_b))


def gmlp_branch(u, v, ln_g, ln_b, ws, bs):
    B, S, _ = u.shape
    nc = S // GM_CHUNK
    v = layer_norm(v, ln_g, ln_b).reshape(B, nc, GM_CHUNK, GM_GROUPS, GM_WIDTH // GM_GROUPS)
    mixed = jnp.einsum('gts,bnsgc->bntgc', ws, v) + bs.T[None, None, :, :, None]
    return u * mixed.reshape(B, S, GM_WIDTH)


def setup_inputs(seed: int = 0) -> dict:
    key = jax.random.key(seed)
    ks = iter(jax.random.split(key, 40))

    def nrm(shape, scale):
        return jax.random.normal(next(ks), shape, jnp.float32) * scale

    def gain(shape):
        return 1.0 + nrm(shape, 0.05)

    L, D = DEPTH, D_MODEL
    x = jax.random.normal(next(ks), (BATCH, SEQ, D), jnp.float32)
    offs = jax.random.randint(next(ks), (BATCH, 1), 0, SEQ, dtype=jnp.int32)
    positions = jnp.arange(SEQ, dtype=jnp.int32)[None, :] + offs
    return {
        "x": x,
        "positions": positions,
        "norm_mix_pre": gain((L, D)),
        "w_in": nrm((L, D, N_IN), D ** -0.5),
        "mla_q_norm": gain((L, Q_LORA)),
        "w_uq": nrm((L, Q_LORA, MLA_HEADS * (QK_NOPE + QK_ROPE)), Q_LORA ** -0.5),
        "mla_kv_norm": gain((L, KV_LORA)),
        "w_ukv": nrm((L, KV_LORA, MLA_HEADS * (QK_NOPE + V_HEAD)), KV_LORA ** -0.5),
        "w_o_mla": nrm((L, MLA_HEADS * V_HEAD, D), (MLA_HEADS * V_HEAD) ** -0.5),
        "sc_conv_w": nrm((L, SC_K, SC_WIDTH), SC_K ** -0.5),
        "w_o_sc": nrm((L, SC_WIDTH, D), SC_WIDTH ** -0.5),
        "cf_conv_w": nrm((L, CF_K, CF_WIDTH), CF_K ** -0.5),
        "cf_conv_b": nrm((L, CF_WIDTH), 0.02),
        "cf_ln_g": gain((L, CF_WIDTH)),
        "cf_ln_b": nrm((L, CF_WIDTH), 0.02),
        "w_o_cf": nrm((L, CF_WIDTH, D), CF_WIDTH ** -0.5),
        "gm_ln_g": gain((L, GM_WIDTH)),
        "gm_ln_b": nrm((L, GM_WIDTH), 0.02),
        "gm_ws": nrm((L, GM_GROUPS, GM_CHUNK, GM_CHUNK), GM_CHUNK ** -0.5),
        "gm_bs": 1.0 + nrm((L, GM_GROUPS, GM_CHUNK), 0.02),
        "w_o_gm": nrm((L, GM_WIDTH, D), GM_WIDTH ** -0.5),
        "gate_b": nrm((L, N_BRANCH, D), 0.02),
        "w_out": nrm((L, D, D), D ** -0.5),
        "norm_mix_post": gain((L, D)),
        "norm_ffn_pre": gain((L, D)),
        "w_ffn_in": nrm((L, D, 2 * FFN_HIDDEN), D ** -0.5),
        "w_ffn_out": nrm((L, FFN_HIDDEN, D), FFN_HIDDEN ** -0.5),
        "norm_ffn_post": gain((L, D)),
    }


def reference(x, positions, norm_mix_pre, w_in, mla_q_norm, w_uq, mla_kv_norm, w_ukv, w_o_mla,
              sc_conv_w, w_o_sc, cf_conv_w, cf_conv_b, cf_ln_g, cf_ln_b, w_o_cf,
              gm_ln_g, gm_ln_b, gm_ws, gm_bs, w_o_gm, gate_b, w_out, norm_mix_post,
              norm_ffn_pre, w_ffn_in, w_ffn_out, norm_ffn_post):
    B, S, D = x.shape
    inv_freq = ROPE_THETA ** (-jnp.arange(0, QK_ROPE, 2, dtype=jnp.float32) / QK_ROPE)
    ang = positions.astype(jnp.float32)[..., None] * inv_freq
    cos, sin = jnp.cos(ang), jnp.sin(ang)
    pts = split_points()

    for l in range(DEPTH):
        h = rms_norm(x, norm_mix_pre[l])
        proj = h @ w_in[l]
        (c_q, c_kv, k_rope, sc_b, sc_c, sc_x, cf_a, cf_g, gm_u, gm_v, gate_logits) = \
            jnp.split(proj, pts, axis=-1)

        y_a = mla_branch(c_q, c_kv, k_rope, cos, sin, mla_q_norm[l], w_uq[l],
                         mla_kv_norm[l], w_ukv[l]) @ w_o_mla[l]
        y_b = short_conv_branch(sc_b, sc_c, sc_x, sc_conv_w[l]) @ w_o_sc[l]
        y_c = conformer_branch(cf_a, cf_g, cf_conv_w[l], cf_conv_b[l],
                               cf_ln_g[l], cf_ln_b[l]) @ w_o_cf[l]
        y_d = gmlp_branch(jax.nn.gelu(gm_u), jax.nn.gelu(gm_v), gm_ln_g[l], gm_ln_b[l],
                          gm_ws[l], gm_bs[l]) @ w_o_gm[l]

        g = jax.nn.sigmoid(gate_logits.reshape(B, S, N_BRANCH, D) + gate_b[l])
        merged = g[:, :, 0] * y_a + g[:, :, 1] * y_b + g[:, :, 2] * y_c + g[:, :, 3] * y_d
        x = x + rms_norm(merged @ w_out[l], norm_mix_post[l])

        h2 = rms_norm(x, norm_ffn_pre[l])
        gu = h2 @ w_ffn_in[l]
        f_g, f_u = jnp.split(gu, 2, axis=-1)
        f = (jax.nn.silu(f_g) * f_u) @ w_ffn_out[l]
        x = x + rms_norm(f, norm_ffn_post[l])
    return x
```

```python
import math
import numpy as np
import concourse.bass as bass
import concourse.mybir as mybir
from concourse.bass_utils import run_bass_kernel_spmd

F32, BF16, I32 = mybir.dt.float32, mybir.dt.bfloat16, mybir.dt.int32
AF = mybir.ActivationFunctionType
ALU = mybir.AluOpType

D = 1024
SEQ = 2048
NSEQ = 2
NT = 512
NTILE = SEQ // NT
HALO = 15
HW = NT + 2 * HALO
HEADS = 8
N_IN = 8224
FFN_H = 2816
EPS = 1e-6
C_Q, C_KV, C_KR, C_SCB, C_SCC, C_SCX, C_CFA, C_CFG, C_GMU, C_GMV, C_GATE = (
    0, 256, 512, 544, 1056, 1568, 2080, 2592, 3104, 3616, 4128)

SAME_ENGINE_SYNC = True


class Sched:
    def __init__(self):
        self.ops = []
        self.last_w = {}
        self.readers = {}
        self.total_keys = set()
        self.label = ''

    def add(self, eng, fn, reads=(), writes=(), dma_key=None, wait_total=False):
        idx = len(self.ops)
        raw, other = set(), set()
        ps_r = [k for k in reads if k[0] == "ps" and k not in writes]
        if ps_r:
            writes = list(writes) + ps_r
        for k in reads:
            w = self.last_w.get(k)
            if w is not None:
                raw.add(w)
        for k in writes:
            w = self.last_w.get(k)
            if w is not None:
                other.add(w)
            other.update(self.readers.get(k, {}).values())
        for k in reads:
            r = self.readers.setdefault(k, {})
            rk = eng if dma_key is None else ("dma", idx)
            r[rk] = idx
        for k in writes:
            self.last_w[k] = idx
            self.readers[k] = {}
        if wait_total:
            self.total_keys.add(dma_key)
            raw = set(d for d in raw if self.ops[d]["dma_key"] != dma_key)
            other = set(d for d in other if self.ops[d]["dma_key"] != dma_key)
        self.ops.append(dict(eng=eng, fn=fn, raw=raw, other=other - raw, dma_key=dma_key, label=self.label))
        return idx

    def emit(self, nc, stack):
        ops = self.ops
        needed = set()
        for op in ops:
            for d in op["raw"] | op["other"]:
                needed.add(d)
        cnt = {}
        dma_cnt = {}
        for i, op in enumerate(ops):
            if op["dma_key"] is not None:
                k = op["dma_key"]
                dma_cnt[k] = dma_cnt.get(k, 0) + 1
                op["sig"] = 16 * dma_cnt[k]
            elif i in needed:
                cnt[op["eng"]] = cnt.get(op["eng"], 0) + 1
                op["sig"] = cnt[op["eng"]]
            else:
                op["sig"] = None
        for op in ops:
            if op["dma_key"] in self.total_keys:
                op["sig"] = 16 * dma_cnt[op["dma_key"]]
        eng_names = ["pe", "act", "dve", "pool", "sp"]
        eng_sem = {e: stack.enter_context(nc.semaphore("e_" + e)) for e in eng_names}
        dma_sem = {}
        for n, k in enumerate(sorted(dma_cnt.keys(), key=str)):
            dma_sem[k] = stack.enter_context(nc.semaphore("d%d" % n))
        by_eng = {e: [] for e in eng_names}
        for i, op in enumerate(ops):
            by_eng[op["eng"]].append(i)

        def run(name, e):
            floor = {}
            for i in by_eng[name]:
                op = ops[i]
                need = {}

                def req(d, is_raw):
                    dop = ops[d]
                    if dop["dma_key"] is None:
                        if dop["eng"] == name:
                            if name == "pe" or not SAME_ENGINE_SYNC:
                                return
                        sem = eng_sem[dop["eng"]]
                    else:
                        sem = dma_sem[dop["dma_key"]]
                    key = id(sem)
                    if dop["sig"] > need.get(key, (None, 0))[1]:
                        need[key] = (sem, dop["sig"])

                for d in op["raw"]:
                    req(d, True)
                for d in op["other"]:
                    req(d, False)
                for key, (sem, val) in need.items():
                    if floor.get(key, 0) >= val:
                        continue
                    e.wait_ge(sem, val)
                    floor[key] = val
                inst = op["fn"](e)
                if inst is None:
                    continue
                try:
                    op["iname"] = inst.ins.name
                except Exception:
                    pass
                if op["dma_key"] is not None:
                    inst.then_inc(dma_sem[op["dma_key"]], 16)
                elif op["sig"] is not None:
                    inst.then_inc(eng_sem[name], 1)

        block = stack.enter_context(nc.Block())

        @block.tensor
        def _(e):
            run("pe", e)

        @block.scalar
        def _(e):
            run("act", e)

        @block.vector
        def _(e):
            run("dve", e)

        @block.gpsimd
        def _(e):
            run("pool", e)

        @block.sync
        def _(e):
            run("sp", e)


class Arena:
    def __init__(self, name, tensor, nbytes, slot=2048):
        self.name, self.t, self.nbytes, self.slot = name, tensor, nbytes, slot

    def view(self, off, dtype, shape):
        esz = 2 if dtype == BF16 else 4
        n = 1
        for s in shape:
            n *= s
        nb = n * esz
        assert off % 4 == 0 and nb % 4 == 0 and off + nb <= self.nbytes, (self.name, off, nb)
        a = self.t[:, off // 4:(off + nb) // 4]
        if dtype != F32:
            a = a.bitcast(dtype)
        if len(shape) == 2:
            a = a.rearrange("p (a b) -> p a b", a=shape[0])
        elif len(shape) == 3:
            a = a.rearrange("p (a b c) -> p a b c", a=shape[0], b=shape[1])
        keys = [(self.name, s) for s in range(off // self.slot, (off + nb - 1) // self.slot + 1)]
        return a, keys


def build_program(layers, n_layers_total=2):
    import os
    _small = os.environ.get("K_SMALL") == "1"
    nc = bass.Bass("TRN2", target_bir_lowering=False)
    S = Sched()
    L = n_layers_total

    def din(name, shape, dt=F32):
        return nc.dram_tensor(name, shape, dt, kind="ExternalInput").ap()

    x_d = din("x", [NSEQ, SEQ, D])
    pos_d = din("positions", [NSEQ, SEQ], I32)
    p_norm_mix_pre = din("norm_mix_pre", [L, D])
    p_w_in = din("w_in", [L, D, N_IN])
    p_q_norm = din("mla_q_norm", [L, 256])
    p_w_uq = din("w_uq", [L, 256, 768])
    p_kv_norm = din("mla_kv_norm", [L, 256])
    p_w_ukv = din("w_ukv", [L, 256, 1024])
    p_w_o_mla = din("w_o_mla", [L, 512, D])
    p_sc_conv_w = din("sc_conv_w", [L, 3, 512])
    p_w_o_sc = din("w_o_sc", [L, 512, D])
    p_cf_conv_w = din("cf_conv_w", [L, 31, 512])
    p_cf_conv_b = din("cf_conv_b", [L, 512])
    p_cf_ln_g = din("cf_ln_g", [L, 512])
    p_cf_ln_b = din("cf_ln_b", [L, 512])
    p_w_o_cf = din("w_o_cf", [L, 512, D])
    p_gm_ln_g = din("gm_ln_g", [L, 512])
    p_gm_ln_b = din("gm_ln_b", [L, 512])
    p_gm_ws = din("gm_ws", [L, 4, 128, 128])
    p_gm_bs = din("gm_bs", [L, 4, 128])
    p_w_o_gm = din("w_o_gm", [L, 512, D])
    p_gate_b = din("gate_b", [L, 4, D])
    p_w_out = din("w_out", [L, D, D])
    p_norm_mix_post = din("norm_mix_post", [L, D])
    p_norm_ffn_pre = din("norm_ffn_pre", [L, D])
    p_w_ffn_in = din("w_ffn_in", [L, D, 2 * FFN_H])
    p_w_ffn_out = din("w_ffn_out", [L, FFN_H, D])
    p_norm_ffn_post = din("norm_ffn_post", [L, D])
    out_d = nc.dram_tensor("out", [NSEQ, SEQ, D], F32, kind="ExternalOutput").ap()

    def dscr(name, shape, dt=BF16):
        return nc.dram_tensor(name, shape, dt, kind="Internal").ap()

    wb = {}
    for l in layers:
        wb[l] = dict(
            win=dscr("b_win%d" % l, [D, N_IN]), wuq=dscr("b_wuq%d" % l, [256, 768]),
            wuqs=dscr("b_wuqs%d" % l, [256, 768]), wukv=dscr("b_wukv%d" % l, [256, 1024]),
            womla=dscr("b_womla%d" % l, [512, D]), wosc=dscr("b_wosc%d" % l, [512, D]),
            wocf=dscr("b_wocf%d" % l, [512, D]), wogm=dscr("b_wogm%d" % l, [512, D]),
            wout=dscr("b_wout%d" % l, [D, D]), wfin=dscr("b_wfin%d" % l, [D, 2 * FFN_H]),
            wfout=dscr("b_wfout%d" % l, [FFN_H, D]), wkrs=dscr("b_wkrs%d" % l, [D, 96]))
    rope_d = dscr("rope_tab", [NSEQ, NTILE, 32, 2 * NT], F32)
    rstd_d = dscr("rstd_row", [SEQ], F32)

    import contextlib
    stack = contextlib.ExitStack()
    with stack:
        def sb(name, shape, dt):
            return stack.enter_context(nc.sbuf_tensor(name, shape, dt))

        xT = sb("xT", [128, 8, SEQ], F32)
        KT = sb("KT", [128, HEADS, SEQ], BF16)
        Vt = sb("Vt", [128, SEQ // 128, HEADS, 65], BF16)
        RING_UNITS = 24
        ring = sb("ring", [128, RING_UNITS * 512], BF16)
        hT = sb("hT", [128, 8, 544], BF16)
        A_BYTES, B_BYTES = 24 * 1024, 24 * 1024
        arA = Arena("A", sb("arenaA", [128, A_BYTES // 4], F32), A_BYTES)
        arB = Arena("B", sb("arenaB", [128, B_BYTES // 4], F32), B_BYTES)
        ident_f = sb("ident_f", [128, 128], F32)
        ident_b = sb("ident_b", [128, 128], BF16)
        ones_b = sb("ones_b", [128, 128], BF16)
        ones_f = sb("ones_f", [128, 128], F32)
        par1 = sb("par1", [128, 128], F32)
        par2 = sb("par2", [128, 128], F32)
        pstage = sb("pstage", [128, 2, 128], F32)
        glnB = sb("glnB", [128, 2, 512], F32)
        wsT = sb("wsT", [128, 4, 128], BF16)
        misc = sb("misc", [128, 16], F32)
        half_gb = sb("half_gb", [128, 8], F32)
        cb2 = sb("cb2", [128, 4], F32)
        diag = [sb("diag%d" % i, [128, 128], BF16) for i in range(6)]
        rwh = sb("rwh", [128, 16], F32)
        zprev_cf = sb("zprev_cf", [128, 4, 16], BF16)
        zprev_sc = sb("zprev_sc", [128, 4, 2], BF16)
        pTb = sb("pTb", [128, 3, NT], BF16)
        psb = [stack.enter_context(nc.psum_tensor("ps%d" % i, [128, 512], F32)) for i in range(8)]

        bank_state = dict(next=0, held=set())

        def next_bank():
            for _ in range(16):
                b = bank_state["next"]
                bank_state["next"] = (b + 1) % 8
                if b not in bank_state["held"]:
                    return b
            raise RuntimeError("no bank")

        def PK(b):
            return ("ps", b)

        def mm(out, lhsT, rhs, start, stop, reads, writes, **kw):
            S.add("pe", lambda e, out=out, lhsT=lhsT, rhs=rhs, start=start, stop=stop, kw=kw:
                  e.matmul(out, lhsT=lhsT, rhs=rhs, start=start, stop=stop, **kw), reads, writes)

        def tr(out, in_, ident, reads, writes):
            S.add("pe", lambda e, out=out, in_=in_, ident=ident: e.transpose(out, in_, ident), reads, writes)

        def act(out, in_, func, reads, writes, bias=None, scale=None):
            kw = {}
            if bias is not None:
                kw["bias"] = bias
            if scale is not None:
                kw["scale"] = scale
            S.add("act", lambda e, out=out, in_=in_, func=func, kw=kw:
                  e.activation(out=out, in_=in_, func=func, **kw), reads, writes)

        def tt(eng, out, in0, in1, op, reads, writes):
            S.add(eng, lambda e, out=out, in0=in0, in1=in1, op=op:
                  e.tensor_tensor(out=out, in0=in0, in1=in1, op=op), reads, writes)

        def ts(eng, out, in0, s1, s2, op0, op1, reads, writes):
            if op1 is None:
                S.add(eng, lambda e, out=out, in0=in0, s1=s1, op0=op0:
                      e.tensor_scalar(out=out, in0=in0, scalar1=s1, scalar2=None, op0=op0), reads, writes)
            else:
                S.add(eng, lambda e, out=out, in0=in0, s1=s1, s2=s2, op0=op0, op1=op1:
                      e.tensor_scalar(out=out, in0=in0, scalar1=s1, scalar2=s2, op0=op0, op1=op1), reads, writes)

        def stt(out, in0, scalar, in1, op0, op1, reads, writes):
            S.add("dve", lambda e, out=out, in0=in0, scalar=scalar, in1=in1, op0=op0, op1=op1:
                  e.scalar_tensor_tensor(out=out, in0=in0, scalar=scalar, in1=in1, op0=op0, op1=op1),
                  reads, writes)

        def cp(eng, out, in_, reads, writes):
            if eng == "act":
                act(out, in_, AF.Copy, reads, writes)
            else:
                S.add(eng, lambda e, out=out, in_=in_: e.tensor_copy(out=out, in_=in_), reads, writes)

        def memset(eng, ap, val, writes):
            S.add(eng, lambda e, ap=ap, val=val: e.memset(ap, val), (), writes)

        def dma(q, out, in_, reads, writes, key, wait_total=False, **kw):
            S.add(q, lambda e, out=out, in_=in_, kw=kw: e.dma_start(out=out, in_=in_, **kw),
                  reads, writes, dma_key=key, wait_total=wait_total)

        evac_rr = dict(i=0)

        def evac_eng():
            evac_rr["i"] ^= 1
            return "act" if evac_rr["i"] else "dve"

        def cast2d(dst, src, key, rows_per=2048):
            R = dst.shape[0]
            r = 0
            while r < R:
                n = min(rows_per, R - r)
                dma("pool", dst[r:r + n, :], src[r:r + n, :], (), [("wd", key)], ("cast", key), wait_total=True)
                r += n

        def flat2d(ap, cols):
            names = " ".join("d%d" % i for i in range(len(ap.shape)))
            f = ap.rearrange("%s -> (%s)" % (names, names))
            return f.rearrange("(r c) -> r c", c=cols)

        def cast_flat(dst, src, key):
            n = 1
            for s in dst.shape:
                n *= s
            c = 2048
            while n % c:
                c -= 1
            cast2d(flat2d(dst, c), flat2d(src, c), key)

        import os
        def emit_casts(l):
            S.label = 'cast'
            w = wb[l]
            cast_flat(w["wukv"], p_w_ukv[l], (l, "wukv"))
            src3 = p_w_in[l]
            dma("pool", w["wkrs"][:, 0:64], src3[:, 448:512], (), [("wd", (l, "wkrs"))], ("cast", (l, "wkrs")), True)
            dma("pool", w["wkrs"][:, 64:80], src3[:, 528:544], (), [("wd", (l, "wkrs"))], ("cast", (l, "wkrs")), True)
            dma("pool", w["wkrs"][:, 80:96], src3[:, 512:528], (), [("wd", (l, "wkrs"))], ("cast", (l, "wkrs")), True)
            cast_flat(w["win"], p_w_in[l], (l, "win"))
            cast_flat(w["wuq"], p_w_uq[l], (l, "wuq"))
            s3 = p_w_uq[l].rearrange("k (h d) -> k h d", h=HEADS)
            d3 = w["wuqs"].rearrange("k (h d) -> k h d", h=HEADS)
            dma("pool", d3[:, :, 0:64], s3[:, :, 0:64], (), [("wd", (l, "wuqs"))], ("cast", (l, "wuqs")), True)
            dma("pool", d3[:, :, 64:80], s3[:, :, 80:96], (), [("wd", (l, "wuqs"))], ("cast", (l, "wuqs")), True)
            dma("pool", d3[:, :, 80:96], s3[:, :, 64:80], (), [("wd", (l, "wuqs"))], ("cast", (l, "wuqs")), True)
            cast_flat(w["womla"], p_w_o_mla[l], (l, "womla"))
            cast_flat(w["wosc"], p_w_o_sc[l], (l, "wosc"))
            cast_flat(w["wocf"], p_w_o_cf[l], (l, "wocf"))
            cast_flat(w["wogm"], p_w_o_gm[l], (l, "wogm"))
            cast_flat(w["wout"], p_w_out[l], (l, "wout"))
            cast_flat(w["wfin"], p_w_ffn_in[l], (l, "wfin"))
            cast_flat(w["wfout"], p_w_ffn_out[l], (l, "wfout"))


        import os
        if int(os.environ.get('K_STAGE', '9')) >= 1:
            emit_casts(layers[0])

        S.label = 'const'
        memset("dve", ones_b[:], 1.0, [("c", "ones_b")])
        memset("dve", ones_f[:], 1.0, [("c", "ones_f")])
        memset("dve", ident_f[:], 0.0, [("c", "ident_f")])
        S.add("pool", lambda e: e.affine_select(out=ident_f[:], in_=ident_f[:], pattern=[[-1, 128]],
                                                compare_op=ALU.not_equal, fill=1.0, base=0,
                                                channel_multiplier=1),
              [("c", "ident_f")], [("c", "ident_f")])
        cp("dve", ident_b[:], ident_f[:], [("c", "ident_f")], [("c", "ident_b")])
        memset("pool", Vt[:, :, :, 64:65], 1.0, [("Vones",)])
        iot = sb("iot", [128, 1], I32)
        S.add("pool", lambda e: e.iota(iot[64:96, :], pattern=[[0, 1]], base=0, channel_multiplier=1),
              (), [("c", "iot")])
        MK = [("c", "misc")]
        cp("dve", misc[64:96, 2:3], iot[64:96, :], [("c", "iot")], MK)
        ts("dve", misc[64:96, 3:4], misc[64:96, 2:3], 16.0, None, ALU.is_ge, None, MK, MK)
        stt(misc[64:96, 2:3], misc[64:96, 3:4], -16.0, misc[64:96, 2:3], ALU.mult, ALU.add, MK, MK)
        act(misc[64:96, 0:1], misc[64:96, 2:3], AF.Exp, MK, MK, scale=-math.log(10000.0) / 16.0)
        ts("dve", misc[64:96, 1:2], misc[64:96, 3:4], 2.0, -1.0, ALU.mult, ALU.add, MK, MK)

        ring_state = dict(pos=0)

        def wload(l, name, k0, nk, n0, nw):
            nel = nk * nw
            nu = (nel + 511) // 512
            pos = ring_state["pos"]
            if pos + nu > RING_UNITS:
                pos = 0
            ring_state["pos"] = pos + nu
            W = wb[l][name]
            src = W[k0 * 128:(k0 + nk) * 128, n0:n0 + nw].rearrange("(k p) n -> p k n", p=128)
            dst = ring[:, pos * 512:pos * 512 + nel].rearrange("p (k n) -> p k n", k=nk)
            keys = [("ring", u) for u in range(pos, pos + nu)]
            dma("sp", dst, src, [("wd", (l, name))], keys, ("ring", pos))
            return dst, keys

        P_PRE, P_POST, P_FPRE, P_FPOST, P_GATEB, P_QN, P_KVN, P_SCW, P_CFB, P_CFG, P_CFBB = (
            0, 8, 16, 24, 32, 64, 66, 68, 80, 84, 88)

        par_calls = dict(n=0)

        def load_params(l):
            S.label = 'params'
            st1, st2 = pstage[:, 0, :], pstage[:, 1, :]
            rk = [("pstage",)]
            memset("dve", pstage[:], 0.0, rk)
            q = "sp"
            par_calls["n"] += 1
            key = ("par", par_calls["n"])
            dma(q, st1[0:124, :], p_cf_conv_w[l].rearrange("k (c p) -> (k c) p", p=128), (), rk, key, True)

            def ld(off, n, src):
                dma(q, st2[off:off + n, :], src, (), rk, key, True)

            ld(P_PRE, 8, p_norm_mix_pre[l].rearrange("(c p) -> c p", p=128))
            ld(P_POST, 8, p_norm_mix_post[l].rearrange("(c p) -> c p", p=128))
            ld(P_FPRE, 8, p_norm_ffn_pre[l].rearrange("(c p) -> c p", p=128))
            ld(P_FPOST, 8, p_norm_ffn_post[l].rearrange("(c p) -> c p", p=128))
            ld(P_GATEB, 32, p_gate_b[l].rearrange("b (c p) -> (b c) p", p=128))
            ld(P_QN, 2, p_q_norm[l].rearrange("(c p) -> c p", p=128))
            ld(P_KVN, 2, p_kv_norm[l].rearrange("(c p) -> c p", p=128))
            ld(P_SCW, 12, p_sc_conv_w[l].rearrange("k (c p) -> (k c) p", p=128))
            ld(P_CFB, 4, p_cf_conv_b[l].rearrange("(c p) -> c p", p=128))
            ld(P_CFG, 4, p_cf_ln_g[l].rearrange("(c p) -> c p", p=128))
            ld(P_CFBB, 4, p_cf_ln_b[l].rearrange("(c p) -> c p", p=128))
            dma(q, glnB[:, 0, :], p_gm_ln_g[l].partition_broadcast(128), (), [("glnB",)], key, True)
            dma(q, glnB[:, 1, :], p_gm_ln_b[l].partition_broadcast(128), (), [("glnB",)], key, True)
            wsl, wk = arA.view(0, F32, [4, 128])
            dma(q, wsl, p_gm_ws[l].rearrange("g t s -> t g s"), (), wk, key, True)
            b0, b1 = next_bank(), next_bank()
            tr(psb[b0][:, 0:128], st1, ident_f[:], rk + [("c", "ident_f")], [PK(b0)])
            tr(psb[b0][:, 128:256], st2, ident_f[:], rk + [("c", "ident_f")], [PK(b0)])
            cp("dve", par1[:], psb[b0][:, 0:128], [PK(b0)], [("par1",)])
            cp("dve", par2[:], psb[b0][:, 128:256], [PK(b0)], [("par2",)])
            for g in range(4):
                tr(psb[b1][:, g * 128:(g + 1) * 128], wsl[:, g, :], ident_f[:], wk + [("c", "ident_f")], [PK(b1)])
            cp("dve", wsT[:].rearrange("p g t -> p (g t)"), psb[b1][:], [PK(b1)], [("wsT",)])
            ts("dve", cb2[:], par2[:, P_CFB:P_CFB + 4], 2.0, None, ALU.mult, None, [("par2",)], [("cb2",)])
            ts("dve", half_gb[:], par2[:, P_CFG:P_CFG + 8], 0.5, None, ALU.mult, None, [("par2",)], [("half_gb",)])

        PAR = [("par1",), ("par2",)]

        def rstd_from_bank(b, n_feat, eps, out_ap, out_keys, lnv, lnv_keys, extra_reads=()):
            act(lnv, psb[b][:], AF.Ln, [PK(b)] + list(extra_reads), lnv_keys, bias=eps_col(eps), scale=1.0 / n_feat)
            act(out_ap, lnv, AF.Exp, lnv_keys, out_keys, scale=-0.5)

        eps_cols = {}

        def eps_col(v):
            if v not in eps_cols:
                i = 4 + len(eps_cols)
                memset("dve", misc[:, i:i + 1], v, [("c", "eps%d" % i)])
                eps_cols[v] = (misc[:, i:i + 1], ("c", "eps%d" % i))
            return eps_cols[v][0]

        for v in (EPS, 4 * EPS):
            eps_col(v)
        EPSK = [k for (_, k) in eps_cols.values()]

        def load_x(s):
            S.label = 'loadx'
            for tb in range(SEQ // 128):
                st, sk = arA.view((tb % 4) * 4096, F32, [1024])
                dma("sp", st, x_d[s, tb * 128:(tb + 1) * 128, :], (), sk, ("xst", tb % 4))
                for half in range(2):
                    b = next_bank()
                    for cc in range(4):
                        c = half * 4 + cc
                        tr(psb[b][:, cc * 128:(cc + 1) * 128], st[:, c * 128:(c + 1) * 128], ident_f[:],
                           sk + [("c", "ident_f")], [PK(b)])
                    cp(evac_eng(), xT[:, half * 4:half * 4 + 4, tb * 128:(tb + 1) * 128],
                       psb[b][:].rearrange("p (c t) -> p c t", c=4), [PK(b)], [("xT", tb // 4, half * 4 + cc) for cc in range(4)])

        out_keys_all = []

        def store_x(s):
            S.label = 'storex'
            for tb in range(SEQ // 128):
                st, sk = arA.view((tb % 4) * 4096, F32, [1024])
                for half in range(2):
                    b = next_bank()
                    for cc in range(4):
                        c = half * 4 + cc
                        tr(psb[b][:, cc * 128:(cc + 1) * 128], xT[:, c, tb * 128:(tb + 1) * 128], ident_f[:],
                           [("xT", tb // 4, c), ("c", "ident_f")], [PK(b)])
                    cp(evac_eng(), st[:, half * 512:(half + 1) * 512], psb[b][:], [PK(b)], sk)
                key = ("ost", s, tb)
                dma("pool", out_d[s, tb * 128:(tb + 1) * 128, :], st, sk, [("out", s, tb)], key)
                out_keys_all.append(("out", s, tb))

        def rope_tables(s):
            S.label = 'rope'
            TWO_PI = 2.0 * math.pi
            C1 = 6.28125
            C2 = TWO_PI - C1
            R = slice(64, 96)
            for j in range(NTILE):
                posi, k0 = arB.view(0, I32, [NT])
                posf, k1 = arB.view(2048, F32, [NT])
                ang, k2 = arB.view(4096, F32, [2, NT])
                ki, k3 = arB.view(8192, I32, [2, NT])
                kf, k4 = arB.view(12288, F32, [2, NT])
                r1, k5 = arB.view(16384, F32, [2, NT])
                tab, k6 = arB.view(20480, F32, [2, NT])
                dma("pool", posi[R, :], pos_d[s, j * NT:(j + 1) * NT].partition_broadcast(32), (), k0, ("posld",))
                cp("dve", posf[R, :], posi[R, :], k0, k1)
                ts("dve", ang[R, 1, :], posf[R, :], misc[R, 0:1], None, ALU.mult, None, k1 + [("c", "misc")], k2)
                ts("dve", ang[R, 0, :], posf[R, :], misc[R, 0:1], math.pi / 2, ALU.mult, ALU.add, k1 + [("c", "misc")], k2)
                ts("dve", kf[R], ang[R], 1.0 / TWO_PI, None, ALU.mult, None, k2, k4)
                cp("dve", ki[R], kf[R], k4, k3)
                cp("dve", kf[R], ki[R], k3, k4)
                stt(r1[R], kf[R], -C1, ang[R], ALU.mult, ALU.add, k4 + k2, k5)
                stt(r1[R], kf[R], -C2, r1[R], ALU.mult, ALU.add, k4 + k5, k5)
                ts("dve", kf[R], r1[R], math.pi, None, ALU.is_gt, None, k5, k4)
                stt(r1[R], kf[R], -TWO_PI, r1[R], ALU.mult, ALU.add, k4 + k5, k5)
                ts("dve", kf[R], r1[R], -math.pi, None, ALU.is_lt, None, k5, k4)
                stt(r1[R], kf[R], TWO_PI, r1[R], ALU.mult, ALU.add, k4 + k5, k5)
                ts("dve", r1[R], r1[R], math.pi, -math.pi, ALU.min, ALU.max, k5, k5)
                act(tab[R, 0, :], r1[R, 0, :], AF.Sin, k5, k6)
                act(tab[R, 1, :], r1[R, 1, :], AF.Sin, k5 + [("c", "misc")], k6, scale=misc[R, 1:2])
                dma("pool", rope_d[s, j].rearrange("r (a t) -> r a t", a=2), tab[R], k6, [("roped", s, j)], ("ropest",))

        def load_rope(s, j, arena, off):
            rp, rk = arena.view(off, F32, [2, NT])
            dma("act", rp[64:96], rope_d[s, j].rearrange("r (a t) -> r a t", a=2), [("roped", s, j)], rk, ("ropeld", arena.name, off))
            return rp, rk

        def h_from_x(j, gcol0, lo, hi, rstd_ap_fn, rstd_keys):
            t0 = j * NT - HALO
            for c in range(8):
                stt(hT[:, c, lo:hi], xT[:, c, t0 + lo:t0 + hi], par2[:, gcol0 + c:gcol0 + c + 1],
                    rstd_ap_fn(t0 + lo, t0 + hi), ALU.mult, ALU.mult,
                    [("xT", jj, c) for jj in set([(t0 + lo) // NT, (t0 + hi - 1) // NT])] + [("par2",)] + rstd_keys,
                    [("hT", c)])

        def sumsq_rstd(src_ap_fn, nchunks, src_keys, sq, sqk, n_feat, eps, out_ap, out_keys, lnv, lnvk):
            for c in range(nchunks):
                act(sq[:, c, :], src_ap_fn(c), AF.Square, src_keys(c), sqk)
            b = next_bank()
            for c in range(nchunks):
                mm(psb[b][:], ones_b[:], sq[:, c, :], c == 0, c == nchunks - 1, sqk + [("c", "ones_b")], [PK(b)])
            rstd_from_bank(b, n_feat, eps, out_ap, out_keys, lnv, lnvk, EPSK)

        def phase1(l, s):
            S.label = 'p1'
            W1, W1k = wload(l, "win", 0, 8, 256, 288)
            Wkrs, Wkrsk = wload(l, "wkrs", 0, 8, 0, 96)
            Wkv, Wkvk = wload(l, "wukv", 0, 2, 0, 1024)
            sq, sqk = arA.view(0, BF16, [8, NT])
            lnv, lnvk = arA.view(8192, F32, [NT])
            ckv, ckvk = arA.view(10240, F32, [2, NT])
            sqkv, sqkvk = arA.view(14336, BF16, [2, NT])
            rkv, rkvk = arA.view(16384, F32, [NT])
            ckvn, ckvnk = arA.view(18432, BF16, [2, NT])
            tA, tAk = arA.view(20480, F32, [NT])
            tB, tBk = arA.view(22528, F32, [NT])
            h1v, h1k = arB.view(8192, BF16, [8, NT])
            hbufs = [(lambda kc: hT[:, kc, HALO:HALO + NT], [("hT", c) for c in range(8)]),
                     (lambda kc: h1v[:, kc, :], h1k)]
            rps = {}

            def part1(j):
                t0 = j * NT
                for c in range(8):
                    act(sq[:, c, :], xT[:, c, t0:t0 + NT], AF.Square, [("xT", j, c)], sqk)

            def part2(j):
                t0 = j * NT
                hf, hfk = hbufs[j % 2]
                rs1, rs1k = arB.view(4096 + (j % 2) * 2048, F32, [NT])
                rps[j] = load_rope(s, j, arB, 0 if j % 2 == 0 else 16384)
                b = next_bank()
                for c in range(8):
                    mm(psb[b][:], ones_b[:], sq[:, c, :], c == 0, c == 7, sqk + [("c", "ones_b")], [PK(b)])
                rstd_from_bank(b, 1024.0, EPS, rs1, rs1k, lnv, lnvk, EPSK)
                for c in range(8):
                    stt(hf(c), xT[:, c, t0:t0 + NT], par2[:, P_PRE + c:P_PRE + c + 1], rs1, ALU.mult, ALU.mult,
                        [("xT", j, c), ("par2",)] + rs1k, [hfk[c]] if j % 2 == 0 else hfk)
                dma("pool", rstd_d[t0:t0 + NT].rearrange("(o n) -> o n", o=1), rs1[0:1, :], rs1k, [("rstdd", j)],
                    ("rstdst", j % 2))

            part1(0)
            part2(0)
            for j in range(NTILE):
                t0 = j * NT
                hm, hk = hbufs[j % 2]
                rp, rpk = rps[j]
                if j + 1 < NTILE:
                    part1(j + 1)
                for cc in range(2):
                    b = next_bank()
                    for kc in range(8):
                        mm(psb[b][:], W1[:, kc, cc * 128:(cc + 1) * 128], hm(kc), kc == 0, kc == 7, hk + W1k, [PK(b)])
                    cp("dve", ckv[:, cc, :], psb[b][:], [PK(b)], ckvk)
                    act(sqkv[:, cc, :], psb[b][:], AF.Square, [PK(b)], sqkvk)
                ba, bb = next_bank(), next_bank()
                bank_state["held"].update((ba, bb))
                for kc in range(8):
                    mm(psb[ba][0:96, :], W1[:, kc, 192:288], hm(kc), kc == 0, kc == 7, hk + W1k, [PK(ba)])
                for kc in range(8):
                    mm(psb[bb][0:96, :], Wkrs[:, kc, 0:96], hm(kc), kc == 0, kc == 7, hk + Wkrsk, [PK(bb)])
                if j + 1 < NTILE:
                    part2(j + 1)
                b = next_bank()
                for cc in range(2):
                    mm(psb[b][:], ones_b[:], sqkv[:, cc, :], cc == 0, cc == 1, sqkvk + [("c", "ones_b")], [PK(b)])
                rstd_from_bank(b, 256.0, EPS, rkv, rkvk, lnv, lnvk, EPSK)
                for cc in range(2):
                    stt(ckvn[:, cc, :], ckv[:, cc, :], par2[:, P_KVN + cc:P_KVN + cc + 1], rkv, ALU.mult, ALU.mult,
                        ckvk + rkvk + [("par2",)], ckvnk)
                R = slice(64, 96)
                tt("dve", tA[R], psb[ba][R, :], rp[R, 0, :], ALU.mult, [PK(ba)] + rpk, tAk)
                tt("dve", tB[R], psb[bb][R, :], rp[R, 1, :], ALU.mult, [PK(bb)] + rpk, tBk)
                bank_state["held"].difference_update((ba, bb))
                tt("pool", KT[R, 0, t0:t0 + NT], tA[R], tB[R], ALU.add, tAk + tBk, [("KTr", j, 0)])
                for h in range(1, HEADS):
                    cp("pool", KT[R, h, t0:t0 + NT], KT[R, 0, t0:t0 + NT], [("KTr", j, 0)], [("KTr", j, h)])
                for h in range(HEADS):
                    b = next_bank()
                    for kc in range(2):
                        mm(psb[b][0:64, :], Wkv[:, kc, h * 128:h * 128 + 64], ckvn[:, kc, :], kc == 0, kc == 1,
                           ckvnk + Wkvk, [PK(b)])
                    cp(evac_eng(), KT[0:64, h, t0:t0 + NT], psb[b][0:64, :], [PK(b)], [("KT", j, h)])
                Wv = Wkv.rearrange("p k (h d) -> p k h d", h=HEADS)
                for tb in range(4):
                    b = next_bank()
                    for kc in range(2):
                        mm(psb[b][:].rearrange("p (h d) -> p h d", h=HEADS), ckvn[:, kc, tb * 128:(tb + 1) * 128],
                           Wv[:, kc, :, 64:128], kc == 0, kc == 1, ckvnk + Wkvk, [PK(b)])
                    cp(evac_eng(), Vt[:, j * 4 + tb, :, 0:64], psb[b][:].rearrange("p (h d) -> p h d", h=HEADS),
                       [PK(b)], [("Vt", j * 4 + tb)])

        SIG_OFFS = [20480, 22528, 0, 2048, 4096]

        def branch_out(l, j, bi, wname, actT, actk, merged, mk):
            S.label = 'p2.bo%d' % bi
            hk = [("hT", c) for c in range(8)]
            NS = len(SIG_OFFS)
            LEAD = NS - 1
            loads = {}
            sigs = {}

            def get_w(pr):
                if pr not in loads:
                    loads[pr] = (wload(l, wname, 0, 4, pr * 256, 256),
                                 wload(l, "win", 0, 8, C_GATE + bi * 1024 + pr * 256, 256))
                return loads[pr]

            def gate(c):
                _, (Wg, Wgk) = get_w(c // 2)
                cc = c % 2
                bg = next_bank()
                for kc in range(8):
                    mm(psb[bg][:], Wg[:, kc, cc * 128:(cc + 1) * 128], hT[:, kc, HALO:HALO + NT], kc == 0, kc == 7,
                       hk + Wgk, [PK(bg)])
                sig, sigk = arB.view(SIG_OFFS[c % NS], F32, [NT])
                hb = half_gateb[:, bi * 8 + c:bi * 8 + c + 1]
                act(sig, psb[bg][:], AF.Tanh, [PK(bg), ("half_gateb",)], sigk, bias=hb, scale=0.5)
                sigs[c] = (sig, sigk)

            def ymerge(c):
                (Wo, Wok), _ = get_w(c // 2)
                cc = c % 2
                by = next_bank()
                for kc in range(4):
                    mm(psb[by][:], Wo[:, kc, cc * 128:(cc + 1) * 128], actT[:, kc, :], kc == 0, kc == 3,
                       actk + Wok, [PK(by)])
                sig, sigk = sigs.pop(c)
                gt, gtk = arA.view(20480 + (c % 2) * 2048, F32, [NT])
                mslice = merged[:, c, :]
                if bi == 0:
                    stt(mslice, sig, 1.0, psb[by][:], ALU.add, ALU.mult, sigk + [PK(by)], [mk[c]])
                else:
                    stt(gt, sig, 1.0, psb[by][:], ALU.add, ALU.mult, sigk + [PK(by)], gtk)
                    tt("dve", mslice, mslice, gt, ALU.add, gtk + [mk[c]], [mk[c]])

            for c in range(LEAD):
                gate(c)
            for c in range(8):
                if c + LEAD < 8:
                    gate(c + LEAD)
                ymerge(c)

        half_gateb = sb("half_gateb", [128, 32], F32)

        diag_state = dict(i=0)

        def dwconv(z, zk, ntap, wcol_fn, bank_for_chunk):
            for c in range(4):
                b = bank_for_chunk(c)
                for k in range(ntap):
                    di = diag_state["i"]
                    diag_state["i"] = (di + 1) % 6
                    dg = diag[di]
                    ts("dve", dg[:], ident_b[:], wcol_fn(k, c), None, ALU.mult, None,
                       [("c", "ident_b")] + PAR, [("diag", di)])
                    mm(psb[b][:], dg[:], z[:, c, k:k + NT], k == 0, k == ntap - 1, zk + [("diag", di)], [PK(b)])

        def prep_h(j):
            lab_save = S.label
            S.label = 'p2.h'
            t0 = j * NT
            hk = [("hT", c) for c in range(8)]
            rwm, rwmk = arA.view(22528, F32, [NT])
            dma("act", rwm, rstd_d[t0:t0 + NT].partition_broadcast(128), [("rstdd", j)], rwmk, ("rstdld", "m"))
            for c in range(8):
                stt(hT[:, c, HALO:HALO + NT], xT[:, c, t0:t0 + NT], par2[:, P_PRE + c:P_PRE + c + 1], rwm,
                    ALU.mult, ALU.mult, [("xT", j, c), ("par2",)] + rwmk, [("hT", c)])
            if j < NTILE - 1:
                dma("act", rwh[:, 0:HALO], rstd_d[t0 + NT:t0 + NT + HALO].partition_broadcast(128), [("rstdd", j + 1)],
                    [("rwh",)], ("rstdld", "h"))
                for c in range(8):
                    stt(hT[:, c, HALO + NT:HW], xT[:, c, t0 + NT:t0 + NT + HALO], par2[:, P_PRE + c:P_PRE + c + 1],
                        rwh[:, 0:HALO], ALU.mult, ALU.mult, [("xT", j + 1, c), ("par2",), ("rwh",)], [("hT", c)])
            else:
                memset("pool", hT[:, :, HALO + NT:HW], 0.0, hk)
            S.label = lab_save

        def phase2(l, s, j):
            t0 = j * NT
            S.label = 'p2.h'
            hk = [("hT", c) for c in range(8)]
            if j == 0:
                prep_h(0)
            hm = lambda kc: hT[:, kc, HALO:HALO + NT]

            S.label = 'p2.q'
            cq, cqk = arB.view(0, F32, [2, NT])
            sqq, sqqk = arB.view(4096, BF16, [2, NT])
            rq, rqk = arB.view(6144, F32, [NT])
            cqn, cqnk = arB.view(8192, BF16, [2, NT])
            oT, oTk = arB.view(10240, BF16, [4, NT])
            rec, reck = arB.view(14336, F32, [4])
            qT, qTk = arA.view(0, BF16, [8, NT])
            on, onk = arA.view(8192, BF16, [4, NT])
            rp, rpk = load_rope(s, j, arA, 14336)
            tA, tAk = arA.view(18432, F32, [NT])
            tB, tBk = arA.view(20480, F32, [NT])
            lnv, lnvk = arA.view(22528, F32, [NT])
            Wq0, Wq0k = wload(l, "win", 0, 8, C_Q, 256)
            for cc in range(2):
                b = next_bank()
                for kc in range(8):
                    mm(psb[b][:], Wq0[:, kc, cc * 128:(cc + 1) * 128], hm(kc), kc == 0, kc == 7, hk + Wq0k, [PK(b)])
                cp("dve", cq[:, cc, :], psb[b][:], [PK(b)], cqk)
                act(sqq[:, cc, :], psb[b][:], AF.Square, [PK(b)], sqqk)
            b = next_bank()
            for cc in range(2):
                mm(psb[b][:], ones_b[:], sqq[:, cc, :], cc == 0, cc == 1, sqqk + [("c", "ones_b")], [PK(b)])
            rstd_from_bank(b, 256.0, EPS, rq, rqk, lnv, lnvk, EPSK)
            for cc in range(2):
                stt(cqn[:, cc, :], cq[:, cc, :], par2[:, P_QN + cc:P_QN + cc + 1], rq, ALU.mult, ALU.mult,
                    cqk + rqk + [("par2",)], cqnk)
            Wuq, Wuqk = wload(l, "wuq", 0, 2, 0, 768)
            Wuqs, Wuqsk = wload(l, "wuqs", 0, 2, 0, 768)
            R = slice(64, 96)
            for h in range(HEADS):
                ba, bb = next_bank(), next_bank()
                for kc in range(2):
                    mm(psb[ba][0:96, :], Wuq[:, kc, h * 96:(h + 1) * 96], cqn[:, kc, :], kc == 0, kc == 1,
                       cqnk + Wuqk, [PK(ba)])
                for kc in range(2):
                    mm(psb[bb][0:96, :], Wuqs[:, kc, h * 96:(h + 1) * 96], cqn[:, kc, :], kc == 0, kc == 1,
                       cqnk + Wuqsk, [PK(bb)])
                cp("act", qT[0:64, h, :], psb[ba][0:64, :], [PK(ba)], [("A", h // 2)])
                tt("dve", tA[R], psb[ba][R, :], rp[R, 0, :], ALU.mult, [PK(ba)] + rpk, tAk)
                tt("dve", tB[R], psb[bb][R, :], rp[R, 1, :], ALU.mult, [PK(bb)] + rpk, tBk)
                tt("dve", qT[R, h, :], tA[R], tB[R], ALU.add, tAk + tBk, [("A", h // 2)])

            S.label = 'p2.attn'
            scale = 96.0 ** -0.5
            NKT = SEQ // 128
            seqn = [(h, kt) for h in range(HEADS) for kt in range(NKT)]
            sbank = {}
            obank = {}

            def emit_S(i):
                h, kt = seqn[i]
                b_ = next_bank()
                sbank[i] = b_
                mm(psb[b_][:], KT[0:96, h, kt * 128:(kt + 1) * 128], qT[0:96, h, :], True, True,
                   [("KT", kt // 4, h), ("KTr", kt // 4, h), ("A", h // 2)], [PK(b_)])

            LOOK = 2
            for i in range(min(LOOK, len(seqn))):
                emit_S(i)
            for i, (h, kt) in enumerate(seqn):
                if kt == 0:
                    bo = next_bank()
                    bank_state["held"].add(bo)
                    obank[h] = bo
                bo = obank[h]
                Ob = psb[bo][:, 0:260].rearrange("p (q d) -> p q d", q=4)
                bs_ = sbank.pop(i)
                pT = pTb[:, i % 3, :]
                pTk = [("pT", i % 3)]
                act(pT, psb[bs_][:], AF.Exp, [PK(bs_)], pTk, scale=scale)
                if i + LOOK < len(seqn):
                    emit_S(i + LOOK)
                for qb in range(4):
                    mm(Ob[:, qb, :], pT[:, qb * 128:(qb + 1) * 128], Vt[:, kt, h, :],
                       (kt == 0 and qb == 0), (kt == NKT - 1),
                       pTk + [("Vt", kt), ("Vones",)], [PK(bo)], skip_group_check=True)
                if kt == NKT - 1:
                    S.add("dve", lambda e, Ob=Ob, rec=rec: e.reciprocal(out=rec, in_=Ob[:, :, 64]),
                          [PK(bo)], reck)
                    for qb in range(4):
                        ts("dve", on[:, qb, h * 64:(h + 1) * 64], Ob[:, qb, 0:64], rec[:, qb:qb + 1], None, ALU.mult, None,
                           [PK(bo)] + reck, [("A", 4 + qb // 2)])
                    bank_state["held"].discard(bo)
            S.label = 'p2.attnT'
            for c in range(4):
                b = next_bank()
                pb = psb[b][:].bitcast(BF16)
                for qb in range(4):
                    tr(pb[:, qb * 128:(qb + 1) * 128], on[:, qb, c * 128:(c + 1) * 128], ident_b[:],
                       [("A", 4 + qb // 2), ("c", "ident_b")], [PK(b)])
                cp(evac_eng(), oT[:, c, :], pb[:, 0:NT], [PK(b)], oTk)

            merged, _mk = arA.view(0, F32, [8, NT])
            mk = [("A", c) for c in range(8)]
            mb, mbk = arA.view(16384, BF16, [8, NT])
            branch_out(l, j, 0, "womla", oT, oTk, merged, mk)

            S.label = 'p2.sc'
            scb, scbk = arB.view(0, BF16, [4, NT])
            zsc, zsck = arB.view(4096, BF16, [4, 516])
            asc, asck = arB.view(10240, BF16, [4, NT])
            ctmp, ctmpk = arB.view(14336, F32, [NT])
            for half in range(2):
                Wb, Wbk = wload(l, "win", 0, 8, C_SCB + half * 256, 256)
                for cc in range(2):
                    c = half * 2 + cc
                    b = next_bank()
                    for kc in range(8):
                        mm(psb[b][:], Wb[:, kc, cc * 128:(cc + 1) * 128], hm(kc), kc == 0, kc == 7, hk + Wbk, [PK(b)])
                    cp(evac_eng(), scb[:, c, :], psb[b][:], [PK(b)], scbk)
            for half in range(2):
                Wc, Wck = wload(l, "win", 0, 8, C_SCC + half * 256, 256)
                Wx, Wxk = wload(l, "win", 0, 8, C_SCX + half * 256, 256)
                for cc in range(2):
                    c = half * 2 + cc
                    wc = lambda kc: Wc[:, kc, cc * 128:(cc + 1) * 128]
                    wx = lambda kc: Wx[:, kc, cc * 128:(cc + 1) * 128]
                    bc, bx = next_bank(), next_bank()
                    for kc in range(8):
                        mm(psb[bc][:], wc(kc), hm(kc), kc == 0, kc == 7, hk + Wck, [PK(bc)])
                    for kc in range(8):
                        mm(psb[bx][:], wx(kc), hm(kc), kc == 0, kc == 7, hk + Wxk, [PK(bx)])
                    cp("act", ctmp, psb[bc][:], [PK(bc)], ctmpk)
                    tt("dve", zsc[:, c, 1:1 + NT], ctmp, psb[bx][:], ALU.mult, ctmpk + [PK(bx)], zsck)
                    bc2, bx2 = next_bank(), next_bank()
                    col = HALO + NT
                    for kc in range(8):
                        mm(psb[bc2][:, 0:2], wc(kc), hT[:, kc, col:col + 2], kc == 0, kc == 7, hk + Wck, [PK(bc2)])
                    for kc in range(8):
                        mm(psb[bx2][:, 0:2], wx(kc), hT[:, kc, col:col + 2], kc == 0, kc == 7, hk + Wxk, [PK(bx2)])
                    cp("act", ctmp[:, 0:2], psb[bc2][:, 0:2], [PK(bc2)], ctmpk)
                    tt("dve", zsc[:, c, 1 + NT:2 + NT], ctmp[:, 0:1], psb[bx2][:, 0:1], ALU.mult, ctmpk + [PK(bx2)], zsck)
            if j == 0:
                memset("pool", zsc[:, :, 0:1], 0.0, zsck)
            else:
                cp("pool", zsc[:, :, 0:1], zprev_sc[:, :, 0:1], [("zprev_sc",)], zsck)
            cp("pool", zprev_sc[:, :, 0:1], zsc[:, :, NT:NT + 1], zsck, [("zprev_sc",)])
            sc_banks = [next_bank() for _ in range(4)]
            dwconv(zsc, zsck, 3, lambda k, c: par2[:, P_SCW + k * 4 + c:P_SCW + k * 4 + c + 1], lambda c: sc_banks[c])
            for c in range(4):
                tt("dve", asc[:, c, :], scb[:, c, :], psb[sc_banks[c]][:], ALU.mult, scbk + [PK(sc_banks[c])], asck)
            branch_out(l, j, 1, "wosc", asc, asck, merged, mk)

            S.label = 'p2.cf'
            zcf, zcfk = arB.view(0, BF16, [4, 544])
            ycf, ycfk = arB.view(6144, F32, [4, NT])
            ybf, ybfk = arB.view(14336, BF16, [4, NT])
            mean, meank = arB.view(18432, F32, [NT])
            ysq, ysqk = arB.view(20480, BF16, [4, NT])
            lnv2, lnv2k = arA.view(16384, F32, [NT])
            rcf, rcfk = arA.view(18432, F32, [NT])
            for half in range(2):
                Wa, Wak = wload(l, "win", 0, 8, C_CFA + half * 256, 256)
                Wg_, Wgk_ = wload(l, "win", 0, 8, C_CFG + half * 256, 256)
                for cc in range(2):
                    c = half * 2 + cc
                    wa = lambda kc: Wa[:, kc, cc * 128:(cc + 1) * 128]
                    wg = lambda kc: Wg_[:, kc, cc * 128:(cc + 1) * 128]
                    ba, bg = next_bank(), next_bank()
                    for kc in range(8):
                        mm(psb[ba][:], wa(kc), hm(kc), kc == 0, kc == 7, hk + Wak, [PK(ba)])
                    for kc in range(8):
                        mm(psb[bg][:], wg(kc), hm(kc), kc == 0, kc == 7, hk + Wgk_, [PK(bg)])
                    act(mean, psb[bg][:], AF.Tanh, [PK(bg)], meank, scale=0.5)
                    stt(zcf[:, c, HALO:HALO + NT], mean, 1.0, psb[ba][:], ALU.add, ALU.mult, meank + [PK(ba)], zcfk)
                    ba2, bg2 = next_bank(), next_bank()
                    col = HALO + NT
                    for kc in range(8):
                        mm(psb[ba2][:, 0:HALO], wa(kc), hT[:, kc, col:col + HALO], kc == 0, kc == 7, hk + Wak, [PK(ba2)])
                    for kc in range(8):
                        mm(psb[bg2][:, 0:HALO], wg(kc), hT[:, kc, col:col + HALO], kc == 0, kc == 7, hk + Wgk_, [PK(bg2)])
                    act(mean[:, 0:16], psb[bg2][:, 0:16], AF.Tanh, [PK(bg2)], meank, scale=0.5)
                    stt(zcf[:, c, HALO + NT:HW], mean[:, 0:HALO], 1.0, psb[ba2][:, 0:HALO], ALU.add, ALU.mult,
                        meank + [PK(ba2)], zcfk)
            if j == 0:
                memset("pool", zcf[:, :, 0:HALO], 0.0, zcfk)
            else:
                cp("pool", zcf[:, :, 0:HALO], zprev_cf[:, :, 0:HALO], [("zprev_cf",)], zcfk)
            cp("pool", zprev_cf[:, :, 0:HALO], zcf[:, :, NT:NT + HALO], zcfk, [("zprev_cf",)])
            cf_banks = [next_bank() for _ in range(4)]
            dwconv(zcf, zcfk, 31, lambda k, c: par1[:, k * 4 + c:k * 4 + c + 1], lambda c: cf_banks[c])
            for c in range(4):
                act(ycf[:, c, :], psb[cf_banks[c]][:], AF.Identity, [PK(cf_banks[c]), ("cb2",)], ycfk, bias=cb2[:, c:c + 1])
                cp("dve", ybf[:, c, :], ycf[:, c, :], ycfk, ybfk)
                act(ysq[:, c, :], ycf[:, c, :], AF.Square, ycfk, ysqk)
            b1, b2 = next_bank(), next_bank()
            for c in range(4):
                mm(psb[b1][:], ones_b[:], ybf[:, c, :], c == 0, c == 3, ybfk + [("c", "ones_b")], [PK(b1)])
            for c in range(4):
                mm(psb[b2][:], ones_b[:], ysq[:, c, :], c == 0, c == 3, ysqk + [("c", "ones_b")], [PK(b2)])
            ts("dve", mean, psb[b1][:], 1.0 / 512.0, None, ALU.mult, None, [PK(b1)], meank)
            msq, msqk = arB.view(0, F32, [NT])
            tt("dve", msq, mean, mean, ALU.mult, meank, msqk)
            stt(lnv2, psb[b2][:], 1.0 / 512.0, msq, ALU.mult, ALU.subtract, [PK(b2)] + msqk, lnv2k)
            act(lnv2, lnv2, AF.Ln, lnv2k + EPSK, lnv2k, bias=eps_col(4 * EPS))
            act(rcf, lnv2, AF.Exp, lnv2k, rcfk, scale=-0.5)
            acf = ybf
            acfk = ybfk
            for c in range(4):
                tt("dve", ycf[:, c, :], ycf[:, c, :], mean, ALU.subtract, ycfk + meank, ycfk)
                tt("dve", ycf[:, c, :], ycf[:, c, :], rcf, ALU.mult, ycfk + rcfk, ycfk)
                ts("dve", ycf[:, c, :], ycf[:, c, :], half_gb[:, c:c + 1], half_gb[:, 4 + c:5 + c], ALU.mult, ALU.add,
                   ycfk + [("half_gb",)], ycfk)
                act(lnv2, ycf[:, c, :], AF.Tanh, ycfk, lnv2k)
                stt(acf[:, c, :], lnv2, 1.0, ycf[:, c, :], ALU.add, ALU.mult, lnv2k + ycfk, acfk)
            branch_out(l, j, 2, "wocf", acf, acfk, merged, mk)

            S.label = 'p2.gm'
            gu, guk = arB.view(0, BF16, [4, NT])
            vn, vnk = arB.view(8192, BF16, [4, NT])
            stt_, sttk = arB.view(12288, F32, [64])
            agm, agmk = arB.view(14336, BF16, [4, NT])
            bsrow, bsrowk = arB.view(18432, F32, [NT])
            dma("act", bsrow[0:1, :], p_gm_bs[l].rearrange("g t -> (g t)").rearrange("(o n) -> o n", o=1), (),
                bsrowk, ("bsld",))
            for half in range(2):
                Wu, Wuk = wload(l, "win", 0, 8, C_GMU + half * 256, 256)
                for cc in range(2):
                    c = half * 2 + cc
                    b = next_bank()
                    for kc in range(8):
                        mm(psb[b][:], Wu[:, kc, cc * 128:(cc + 1) * 128], hm(kc), kc == 0, kc == 7, hk + Wuk, [PK(b)])
                    act(gu[:, c, :], psb[b][:], AF.Gelu_apprx_tanh, [PK(b)], guk)
            Wv_, Wvk_ = wload(l, "win", 0, 8, C_GMV, 512)
            v4 = [arA.view(16384 + i * 2048, F32, [NT]) for i in range(4)]
            for tb in range(4):
                b = next_bank()
                for kc in range(8):
                    mm(psb[b][:], hT[:, kc, HALO + tb * 128:HALO + (tb + 1) * 128], Wv_[:, kc, :], kc == 0, kc == 7,
                       hk + Wvk_, [PK(b)])
                v, vk = v4[tb]
                act(v, psb[b][:], AF.Gelu_apprx_tanh, [PK(b)], vk)
                st6 = stt_[:, 32 + tb * 8:32 + tb * 8 + 6]
                mv = stt_[:, tb * 2:tb * 2 + 2]
                S.add("dve", lambda e, st6=st6, v=v: e.bn_stats(out=st6, in_=v), vk, sttk)
                S.add("dve", lambda e, st6=st6, mv=mv: e.bn_aggr(out=mv, in_=st6), sttk, sttk)
            mv4 = stt_[:, 0:8].rearrange("p (t two) -> p t two", two=2)
            act(stt_[:, 8:12], mv4[:, :, 1], AF.Ln, sttk + EPSK, sttk, bias=eps_col(EPS))
            act(stt_[:, 12:16], stt_[:, 8:12], AF.Exp, sttk, sttk, scale=-0.5)
            gm_banks = [next_bank() for _ in range(4)]
            for tb in range(4):
                v, vk = v4[tb]
                ts("dve", v, v, stt_[:, tb * 2:tb * 2 + 1], stt_[:, 12 + tb:13 + tb], ALU.subtract, ALU.mult, vk + sttk, vk)
                tt("dve", v, v, glnB[:, 0, :], ALU.mult, vk + [("glnB",)], vk)
                tt("dve", vn[:, tb, :], v, glnB[:, 1, :], ALU.add, vk + [("glnB",)], [("vn", tb)])
                for g in range(4):
                    ob = psb[gm_banks[g]][:, tb * 128:(tb + 1) * 128]
                    mm(ob, vn[:, tb, g * 128:(g + 1) * 128], wsT[:, g, :], True, False, [("vn", tb), ("wsT",)],
                       [PK(gm_banks[g])], skip_group_check=True)
                    mm(ob, ones_f[0:1, :], bsrow[0:1, g * 128:(g + 1) * 128], False, True,
                       bsrowk + [("c", "ones_f")], [PK(gm_banks[g])], skip_group_check=True)
            for g in range(4):
                tt("dve", agm[:, g, :], gu[:, g, :], psb[gm_banks[g]][:], ALU.mult, guk + [PK(gm_banks[g])], agmk)
            branch_out(l, j, 3, "wogm", agm, agmk, merged, mk)

            S.label = 'p2.wout'
            for c in range(8):
                cp("act" if c % 2 else "dve", mb[:, c, :], merged[:, c, :], [mk[c]], mbk)
            m, _ = arB.view(0, F32, [8, NT])
            mkk = [("B", c) for c in range(8)]
            msq8, msq8k = arB.view(16384, BF16, [8, NT])
            lnv3, lnv3k = arA.view(0, F32, [NT])
            rp3, rp3k = arA.view(2048, F32, [NT])

            def lin8(wname, rhs, rhsk, nk_total, dst, dstk, sqdst, sqk_):
                for half in range(2):
                    banks = [next_bank() for _ in range(4)]
                    k0 = 0
                    while k0 < nk_total:
                        nk = min(8, nk_total - k0)
                        Wt, Wtk = wload(l, wname, k0, nk, half * 512, 512)
                        for cc in range(4):
                            for kc in range(nk):
                                mm(psb[banks[cc]][:], Wt[:, kc, cc * 128:(cc + 1) * 128], rhs[:, k0 + kc, :],
                                   (k0 + kc) == 0, (k0 + kc) == nk_total - 1, rhsk + Wtk, [PK(banks[cc])],
                                   skip_group_check=True)
                        k0 += nk
                    for cc in range(4):
                        c = half * 4 + cc
                        cp("dve", dst[:, c, :], psb[banks[cc]][:], [PK(banks[cc])], [dstk[c]])
                        act(sqdst[:, c, :], psb[banks[cc]][:], AF.Square, [PK(banks[cc])], sqk_)

            def post_norm_residual(gcol, eps, src, srck):
                b = next_bank()
                for c in range(8):
                    mm(psb[b][:], ones_b[:], msq8[:, c, :], c == 0, c == 7, msq8k + [("c", "ones_b")], [PK(b)])
                rstd_from_bank(b, 1024.0, eps, rp3, rp3k, lnv3, lnv3k, EPSK)
                for c in range(8):
                    stt(src[:, c, :], src[:, c, :], par2[:, gcol + c:gcol + c + 1], rp3, ALU.mult, ALU.mult,
                        [srck[c], ("par2",)] + rp3k, [srck[c]])
                    tt("dve", xT[:, c, t0:t0 + NT], xT[:, c, t0:t0 + NT], src[:, c, :], ALU.add,
                       [srck[c], ("xT", j, c)], [("xT", j, c)])

            lin8("wout", mb, mbk, 8, m, mkk, msq8, msq8k)
            post_norm_residual(P_POST, 4 * EPS, m, mkk)

            S.label = 'p2.ffn_in'
            sq, sqk = arA.view(0, BF16, [8, NT])
            rf, rfk = arA.view(8192, F32, [NT])
            lnv4, lnv4k = arA.view(10240, F32, [NT])
            sumsq_rstd(lambda c: xT[:, c, t0:t0 + NT], 8, lambda c: [("xT", j, c)], sq, sqk, 1024.0, EPS,
                       rf, rfk, lnv4, lnv4k)
            for c in range(8):
                stt(hT[:, c, HALO:HALO + NT], xT[:, c, t0:t0 + NT], par2[:, P_FPRE + c:P_FPRE + c + 1], rf,
                    ALU.mult, ALU.mult, [("xT", j, c), ("par2",)] + rfk, [("hT", c)])
            f, _ = arA.view(0, BF16, [22, NT])
            fk = [("A", i) for i in range(11)]
            for blk in range(FFN_H // 256):
                n0 = blk * 256
                Wg, Wgk = wload(l, "wfin", 0, 8, n0, 256)
                Wu2, Wu2k = wload(l, "wfin", 0, 8, FFN_H + n0, 256)
                for cc in range(2):
                    hc = blk * 2 + cc
                    bg, bu = next_bank(), next_bank()
                    for kc in range(8):
                        mm(psb[bg][:], Wg[:, kc, cc * 128:(cc + 1) * 128], hm(kc), kc == 0, kc == 7, hk + Wgk, [PK(bg)])
                    for kc in range(8):
                        mm(psb[bu][:], Wu2[:, kc, cc * 128:(cc + 1) * 128], hm(kc), kc == 0, kc == 7, hk + Wu2k, [PK(bu)])
                    th, thk = arB.view(20480 + (hc % 2) * 2048, F32, [NT])
                    aa, aak = arA.view(22528, F32, [NT])
                    act(th, psb[bg][:], AF.Tanh, [PK(bg)], thk, scale=0.5)
                    stt(aa, th, 1.0, psb[bg][:], ALU.add, ALU.mult, thk + [PK(bg)], aak)
                    tt("dve", f[:, hc, :], aa, psb[bu][:], ALU.mult, aak + [PK(bu)], [("A", hc // 2)])
            if j + 1 < NTILE:
                prep_h(j + 1)
            S.label = 'p2.ffn_out'
            m2, _ = arB.view(0, F32, [8, NT])
            lin8("wfout", f, fk, 22, m2, mkk, msq8, msq8k)
            post_norm_residual(P_FPOST, 4 * EPS, m2, mkk)

        import os
        STAGE = int(os.environ.get("K_STAGE", "9"))
        for s in range(NSEQ):
            load_x(s)
            if STAGE >= 2:
                rope_tables(s)
            for l in layers:
                if STAGE >= 2:
                    load_params(l)
                    ts("dve", half_gateb[:], par2[:, P_GATEB:P_GATEB + 32], 0.5, None, ALU.mult, None, [("par2",)],
                       [("half_gateb",)])
                if STAGE >= 3:
                    phase1(l, s)
                if s == 0 and l == layers[0] and STAGE >= 1:
                    for l2 in layers[1:]:
                        emit_casts(l2)
                if STAGE >= 4:
                    for j in range(NTILE):
                        phase2(l, s, j)
            store_x(s)
        S.add("sp", lambda e: None, out_keys_all, ())
        S.emit(nc, stack)
    nc._sched = S
    return nc


_PARAM_NAMES = ["norm_mix_pre", "w_in", "mla_q_norm", "w_uq", "mla_kv_norm", "w_ukv", "w_o_mla", "sc_conv_w",
                "w_o_sc", "cf_conv_w", "cf_conv_b", "cf_ln_g", "cf_ln_b", "w_o_cf", "gm_ln_g", "gm_ln_b", "gm_ws",
                "gm_bs", "w_o_gm", "gate_b", "w_out", "norm_mix_post", "norm_ffn_pre", "w_ffn_in", "w_ffn_out",
                "norm_ffn_post"]

FUSED = True


def kernel(**inputs):
    x = np.ascontiguousarray(np.asarray(inputs["x"], dtype=np.float32))
    pos = np.ascontiguousarray(np.asarray(inputs["positions"], dtype=np.int32))
    params = {k: np.ascontiguousarray(np.asarray(inputs[k], dtype=np.float32)) for k in _PARAM_NAMES}
    n = 8
    launches = [[0, 1]] if FUSED else [[0], [1]]
    cur = x
    for layers in launches:
        nc = build_program(layers)
        in_maps = []
        for c in range(n):
            m = {"x": np.ascontiguousarray(cur[c * NSEQ:(c + 1) * NSEQ]),
                 "positions": np.ascontiguousarray(pos[c * NSEQ:(c + 1) * NSEQ])}
            m.update(params)
            in_maps.append(m)
        res = run_bass_kernel_spmd(nc, in_maps, core_ids=list(range(n)))
        cur = np.concatenate([np.asarray(r["out"]) for r in res.results], axis=0).astype(np.float32)
    return cur
```

```python
import math
import numpy as np
import concourse.bass as bass
import concourse.mybir as mybir
from concourse.bass_utils import run_bass_kernel_spmd

F32, BF16, I32 = mybir.dt.float32, mybir.dt.bfloat16, mybir.dt.int32
AF = mybir.ActivationFunctionType
ALU = mybir.AluOpType

D = 1024
SEQ = 2048
NSEQ = 2
NT = 512
NTILE = SEQ // NT
HALO = 15
HW = NT + 2 * HALO
HEADS = 8
N_IN = 8224
FFN_H = 2816
EPS = 1e-6
C_Q, C_KV, C_KR, C_SCB, C_SCC, C_SCX, C_CFA, C_CFG, C_GMU, C_GMV, C_GATE = (
    0, 256, 512, 544, 1056, 1568, 2080, 2592, 3104, 3616, 4128)

SAME_ENGINE_SYNC = True


class Sched:
    def __init__(self):
        self.ops = []
        self.last_w = {}
        self.readers = {}
        self.total_keys = set()
        self.label = ''

    def add(self, eng, fn, reads=(), writes=(), dma_key=None, wait_total=False):
        idx = len(self.ops)
        raw, other = set(), set()
        ps_r = [k for k in reads if k[0] == "ps" and k not in writes]
        if ps_r:
            writes = list(writes) + ps_r
        for k in reads:
            w = self.last_w.get(k)
            if w is not None:
                raw.add(w)
        for k in writes:
            w = self.last_w.get(k)
            if w is not None:
                other.add(w)
            other.update(self.readers.get(k, {}).values())
        for k in reads:
            r = self.readers.setdefault(k, {})
            rk = eng if dma_key is None else ("dma", idx)
            r[rk] = idx
        for k in writes:
            self.last_w[k] = idx
            self.readers[k] = {}
        if wait_total:
            self.total_keys.add(dma_key)
            raw = set(d for d in raw if self.ops[d]["dma_key"] != dma_key)
            other = set(d for d in other if self.ops[d]["dma_key"] != dma_key)
        self.ops.append(dict(eng=eng, fn=fn, raw=raw, other=other - raw, dma_key=dma_key, label=self.label))
        return idx

    def emit(self, nc, stack):
        ops = self.ops
        needed = set()
        for op in ops:
            for d in op["raw"] | op["other"]:
                needed.add(d)
        cnt = {}
        dma_cnt = {}
        for i, op in enumerate(ops):
            if op["dma_key"] is not None:
                k = op["dma_key"]
                dma_cnt[k] = dma_cnt.get(k, 0) + 1
                op["sig"] = 16 * dma_cnt[k]
            elif i in needed:
                cnt[op["eng"]] = cnt.get(op["eng"], 0) + 1
                op["sig"] = cnt[op["eng"]]
            else:
                op["sig"] = None
        for op in ops:
            if op["dma_key"] in self.total_keys:
                op["sig"] = 16 * dma_cnt[op["dma_key"]]
        eng_names = ["pe", "act", "dve", "pool", "sp"]
        eng_sem = {e: stack.enter_context(nc.semaphore("e_" + e)) for e in eng_names}
        dma_sem = {}
        for n, k in enumerate(sorted(dma_cnt.keys(), key=str)):
            dma_sem[k] = stack.enter_context(nc.semaphore("d%d" % n))
        by_eng = {e: [] for e in eng_names}
        for i, op in enumerate(ops):
            by_eng[op["eng"]].append(i)

        def run(name, e):
            floor = {}
            for i in by_eng[name]:
                op = ops[i]
                need = {}

                def req(d, is_raw):
                    dop = ops[d]
                    if dop["dma_key"] is None:
                        if dop["eng"] == name:
                            if name == "pe" or not SAME_ENGINE_SYNC:
                                return
                        sem = eng_sem[dop["eng"]]
                    else:
                        sem = dma_sem[dop["dma_key"]]
                    key = id(sem)
                    if dop["sig"] > need.get(key, (None, 0))[1]:
                        need[key] = (sem, dop["sig"])

                for d in op["raw"]:
                    req(d, True)
                for d in op["other"]:
                    req(d, False)
                for key, (sem, val) in need.items():
                    if floor.get(key, 0) >= val:
                        continue
                    e.wait_ge(sem, val)
                    floor[key] = val
                inst = op["fn"](e)
                if inst is None:
                    continue
                try:
                    op["iname"] = inst.ins.name
                except Exception:
                    pass
                if op["dma_key"] is not None:
                    inst.then_inc(dma_sem[op["dma_key"]], 16)
                elif op["sig"] is not None:
                    inst.then_inc(eng_sem[name], 1)

        block = stack.enter_context(nc.Block())

        @block.tensor
        def _(e):
            run("pe", e)

        @block.scalar
        def _(e):
            run("act", e)

        @block.vector
        def _(e):
            run("dve", e)

        @block.gpsimd
        def _(e):
            run("pool", e)

        @block.sync
        def _(e):
            run("sp", e)


class Arena:
    def __init__(self, name, tensor, nbytes, slot=2048):
        self.name, self.t, self.nbytes, self.slot = name, tensor, nbytes, slot

    def view(self, off, dtype, shape):
        esz = 2 if dtype == BF16 else 4
        n = 1
        for s in shape:
            n *= s
        nb = n * esz
        assert off % 4 == 0 and nb % 4 == 0 and off + nb <= self.nbytes, (self.name, off, nb)
        a = self.t[:, off // 4:(off + nb) // 4]
        if dtype != F32:
            a = a.bitcast(dtype)
        if len(shape) == 2:
            a = a.rearrange("p (a b) -> p a b", a=shape[0])
        elif len(shape) == 3:
            a = a.rearrange("p (a b c) -> p a b c", a=shape[0], b=shape[1])
        keys = [(self.name, s) for s in range(off // self.slot, (off + nb - 1) // self.slot + 1)]
        return a, keys


def build_program(layers, n_layers_total=2):
    import os
    _small = os.environ.get("K_SMALL") == "1"
    nc = bass.Bass("TRN2", target_bir_lowering=False)
    S = Sched()
    L = n_layers_total

    def din(name, shape, dt=F32):
        return nc.dram_tensor(name, shape, dt, kind="ExternalInput").ap()

    x_d = din("x", [NSEQ, SEQ, D])
    pos_d = din("positions", [NSEQ, SEQ], I32)
    p_norm_mix_pre = din("norm_mix_pre", [L, D])
    p_w_in = din("w_in", [L, D, N_IN])
    p_q_norm = din("mla_q_norm", [L, 256])
    p_w_uq = din("w_uq", [L, 256, 768])
    p_kv_norm = din("mla_kv_norm", [L, 256])
    p_w_ukv = din("w_ukv", [L, 256, 1024])
    p_w_o_mla = din("w_o_mla", [L, 512, D])
    p_sc_conv_w = din("sc_conv_w", [L, 3, 512])
    p_w_o_sc = din("w_o_sc", [L, 512, D])
    p_cf_conv_w = din("cf_conv_w", [L, 31, 512])
    p_cf_conv_b = din("cf_conv_b", [L, 512])
    p_cf_ln_g = din("cf_ln_g", [L, 512])
    p_cf_ln_b = din("cf_ln_b", [L, 512])
    p_w_o_cf = din("w_o_cf", [L, 512, D])
    p_gm_ln_g = din("gm_ln_g", [L, 512])
    p_gm_ln_b = din("gm_ln_b", [L, 512])
    p_gm_ws = din("gm_ws", [L, 4, 128, 128])
    p_gm_bs = din("gm_bs", [L, 4, 128])
    p_w_o_gm = din("w_o_gm", [L, 512, D])
    p_gate_b = din("gate_b", [L, 4, D])
    p_w_out = din("w_out", [L, D, D])
    p_norm_mix_post = din("norm_mix_post", [L, D])
    p_norm_ffn_pre = din("norm_ffn_pre", [L, D])
    p_w_ffn_in = din("w_ffn_in", [L, D, 2 * FFN_H])
    p_w_ffn_out = din("w_ffn_out", [L, FFN_H, D])
    p_norm_ffn_post = din("norm_ffn_post", [L, D])
    out_d = nc.dram_tensor("out", [NSEQ, SEQ, D], F32, kind="ExternalOutput").ap()

    def dscr(name, shape, dt=BF16):
        return nc.dram_tensor(name, shape, dt, kind="Internal").ap()

    wb = {}
    for l in layers:
        wb[l] = dict(
            win=dscr("b_win%d" % l, [D, N_IN]), wuq=dscr("b_wuq%d" % l, [256, 768]),
            wuqs=dscr("b_wuqs%d" % l, [256, 768]), wukv=dscr("b_wukv%d" % l, [256, 1024]),
            womla=dscr("b_womla%d" % l, [512, D]), wosc=dscr("b_wosc%d" % l, [512, D]),
            wocf=dscr("b_wocf%d" % l, [512, D]), wogm=dscr("b_wogm%d" % l, [512, D]),
            wout=dscr("b_wout%d" % l, [D, D]), wfin=dscr("b_wfin%d" % l, [D, 2 * FFN_H]),
            wfout=dscr("b_wfout%d" % l, [FFN_H, D]), wkrs=dscr("b_wkrs%d" % l, [D, 96]))
    rope_d = dscr("rope_tab", [NSEQ, NTILE, 32, 2 * NT], F32)
    rstd_d = dscr("rstd_row", [SEQ], F32)

    import contextlib
    stack = contextlib.ExitStack()
    with stack:
        def sb(name, shape, dt):
            return stack.enter_context(nc.sbuf_tensor(name, shape, dt))

        xT = sb("xT", [128, 8, SEQ], F32)
        KT = sb("KT", [128, HEADS, SEQ], BF16)
        Vt = sb("Vt", [128, SEQ // 128, HEADS, 65], BF16)
        RING_UNITS = 24
        ring = sb("ring", [128, RING_UNITS * 512], BF16)
        hT = sb("hT", [128, 8, 544], BF16)
        A_BYTES, B_BYTES = 24 * 1024, 24 * 1024
        arA = Arena("A", sb("arenaA", [128, A_BYTES // 4], F32), A_BYTES)
        arB = Arena("B", sb("arenaB", [128, B_BYTES // 4], F32), B_BYTES)
        ident_f = sb("ident_f", [128, 128], F32)
        ident_b = sb("ident_b", [128, 128], BF16)
        ones_b = sb("ones_b", [128, 128], BF16)
        ones_f = sb("ones_f", [128, 128], F32)
        par1 = sb("par1", [128, 128], F32)
        par2 = sb("par2", [128, 128], F32)
        pstage = sb("pstage", [128, 2, 128], F32)
        glnB = sb("glnB", [128, 2, 512], F32)
        wsT = sb("wsT", [128, 4, 128], BF16)
        misc = sb("misc", [128, 16], F32)
        half_gb = sb("half_gb", [128, 8], F32)
        cb2 = sb("cb2", [128, 4], F32)
        diag = [sb("diag%d" % i, [128, 128], BF16) for i in range(6)]
        rwh = sb("rwh", [128, 16], F32)
        zprev_cf = sb("zprev_cf", [128, 4, 16], BF16)
        zprev_sc = sb("zprev_sc", [128, 4, 2], BF16)
        pTb = sb("pTb", [128, 3, NT], BF16)
        psb = [stack.enter_context(nc.psum_tensor("ps%d" % i, [128, 512], F32)) for i in range(8)]

        bank_state = dict(next=0, held=set())

        def next_bank():
            for _ in range(16):
                b = bank_state["next"]
                bank_state["next"] = (b + 1) % 8
                if b not in bank_state["held"]:
                    return b
            raise RuntimeError("no bank")

        def PK(b):
            return ("ps", b)

        def mm(out, lhsT, rhs, start, stop, reads, writes, **kw):
            S.add("pe", lambda e, out=out, lhsT=lhsT, rhs=rhs, start=start, stop=stop, kw=kw:
                  e.matmul(out, lhsT=lhsT, rhs=rhs, start=start, stop=stop, **kw), reads, writes)

        def tr(out, in_, ident, reads, writes):
            S.add("pe", lambda e, out=out, in_=in_, ident=ident: e.transpose(out, in_, ident), reads, writes)

        def act(out, in_, func, reads, writes, bias=None, scale=None):
            kw = {}
            if bias is not None:
                kw["bias"] = bias
            if scale is not None:
                kw["scale"] = scale
            S.add("act", lambda e, out=out, in_=in_, func=func, kw=kw:
                  e.activation(out=out, in_=in_, func=func, **kw), reads, writes)

        def tt(eng, out, in0, in1, op, reads, writes):
            S.add(eng, lambda e, out=out, in0=in0, in1=in1, op=op:
                  e.tensor_tensor(out=out, in0=in0, in1=in1, op=op), reads, writes)

        def ts(eng, out, in0, s1, s2, op0, op1, reads, writes):
            if op1 is None:
                S.add(eng, lambda e, out=out, in0=in0, s1=s1, op0=op0:
                      e.tensor_scalar(out=out, in0=in0, scalar1=s1, scalar2=None, op0=op0), reads, writes)
            else:
                S.add(eng, lambda e, out=out, in0=in0, s1=s1, s2=s2, op0=op0, op1=op1:
                      e.tensor_scalar(out=out, in0=in0, scalar1=s1, scalar2=s2, op0=op0, op1=op1), reads, writes)

        def stt(out, in0, scalar, in1, op0, op1, reads, writes):
            S.add("dve", lambda e, out=out, in0=in0, scalar=scalar, in1=in1, op0=op0, op1=op1:
                  e.scalar_tensor_tensor(out=out, in0=in0, scalar=scalar, in1=in1, op0=op0, op1=op1),
                  reads, writes)

        def cp(eng, out, in_, reads, writes):
            if eng == "act":
                act(out, in_, AF.Copy, reads, writes)
            else:
                S.add(eng, lambda e, out=out, in_=in_: e.tensor_copy(out=out, in_=in_), reads, writes)

        def memset(eng, ap, val, writes):
            S.add(eng, lambda e, ap=ap, val=val: e.memset(ap, val), (), writes)

        def dma(q, out, in_, reads, writes, key, wait_total=False, **kw):
            S.add(q, lambda e, out=out, in_=in_, kw=kw: e.dma_start(out=out, in_=in_, **kw),
                  reads, writes, dma_key=key, wait_total=wait_total)

        evac_rr = dict(i=0)

        def evac_eng():
            evac_rr["i"] ^= 1
            return "act" if evac_rr["i"] else "dve"

        def cast2d(dst, src, key, rows_per=2048):
            R = dst.shape[0]
            r = 0
            while r < R:
                n = min(rows_per, R - r)
                dma("pool", dst[r:r + n, :], src[r:r + n, :], cast_gate, [("wd", key)], ("cast", key), wait_total=True)
                r += n

        def flat2d(ap, cols):
            names = " ".join("d%d" % i for i in range(len(ap.shape)))
            f = ap.rearrange("%s -> (%s)" % (names, names))
            return f.rearrange("(r c) -> r c", c=cols)

        def cast_flat(dst, src, key):
            n = 1
            for s in dst.shape:
                n *= s
            c = 2048
            while n % c:
                c -= 1
            cast2d(flat2d(dst, c), flat2d(src, c), key)

        import os
        WIN_B = [0, 256, 544, 1056, 2080, 3104, 4128, 5152, 6176, 7200, 8224]
        WFIN_B = [0, 1024, 2048, 2816, 2816 + 1024, 2816 + 2048, 5632]

        def wgroup(name, k0, n0, nw):
            if name == "win":
                B = WIN_B
            elif name == "wfin":
                B = WFIN_B
            elif name == "wfout":
                return k0 // 8
            else:
                return 0
            for g in range(len(B) - 1):
                if B[g] <= n0 < B[g + 1]:
                    assert n0 + nw <= B[g + 1], (name, n0, nw)
                    return g % 3 if name == "wfin" else g
            raise AssertionError((name, n0))

        CAST_ALIAS = {"wkrs": ("win", 1), "wukv": ("win", 1), "wuq": ("win", 0), "wuqs": ("win", 0),
                      "womla": ("win", 6), "wosc": ("win", 7), "wocf": ("win", 8), "wogm": ("win", 9)}

        def wkey(l, name, g):
            if name in CAST_ALIAS:
                return (l,) + CAST_ALIAS[name]
            return (l, name, g)

        cast_gate = []

        def cast_cols(dst, src, c0, c1, key):
            dma("pool", dst[:, c0:c1], src[:, c0:c1], cast_gate, [("wd", key)], ("cast", key), wait_total=True)

        def emit_casts(l):
            S.label = 'cast'
            w = wb[l]
            src3 = p_w_in[l]

            def win(g):
                cast_cols(w["win"], src3, WIN_B[g], WIN_B[g + 1], (l, "win", g))

            win(1)
            kk = wkey(l, "wkrs", 0)
            dma("pool", w["wkrs"][:, 0:64], src3[:, 448:512], (), [("wd", kk)], ("cast", kk), True)
            dma("pool", w["wkrs"][:, 64:80], src3[:, 528:544], (), [("wd", kk)], ("cast", kk), True)
            dma("pool", w["wkrs"][:, 80:96], src3[:, 512:528], (), [("wd", kk)], ("cast", kk), True)
            cast_flat(w["wukv"], p_w_ukv[l], wkey(l, "wukv", 0))
            win(0)
            cast_flat(w["wuq"], p_w_uq[l], wkey(l, "wuq", 0))
            s3 = p_w_uq[l].rearrange("k (h d) -> k h d", h=HEADS)
            d3 = w["wuqs"].rearrange("k (h d) -> k h d", h=HEADS)
            kk = wkey(l, "wuqs", 0)
            dma("pool", d3[:, :, 0:64], s3[:, :, 0:64], (), [("wd", kk)], ("cast", kk), True)
            dma("pool", d3[:, :, 64:80], s3[:, :, 80:96], (), [("wd", kk)], ("cast", kk), True)
            dma("pool", d3[:, :, 80:96], s3[:, :, 64:80], (), [("wd", kk)], ("cast", kk), True)
            cast_flat(w["womla"], p_w_o_mla[l], wkey(l, "womla", 0))
            win(6)
            win(2)
            win(3)
            cast_flat(w["wosc"], p_w_o_sc[l], wkey(l, "wosc", 0))
            win(7)
            win(4)
            cast_flat(w["wocf"], p_w_o_cf[l], wkey(l, "wocf", 0))
            win(8)
            win(5)
            cast_flat(w["wogm"], p_w_o_gm[l], wkey(l, "wogm", 0))
            win(9)
            cast_flat(w["wout"], p_w_out[l], (l, "wout", 0))
            for g in (0, 3, 1, 4, 2, 5):
                cast_cols(w["wfin"], p_w_ffn_in[l], WFIN_B[g], WFIN_B[g + 1], (l, "wfin", g % 3))
            for g, (r0, r1) in enumerate(((0, 1024), (1024, 2048), (2048, FFN_H))):
                cast_flat(w["wfout"][r0:r1, :], p_w_ffn_out[l][r0:r1, :], (l, "wfout", g))


        S.label = 'const'
        memset("dve", ones_b[:], 1.0, [("c", "ones_b")])
        memset("dve", ones_f[:], 1.0, [("c", "ones_f")])
        memset("dve", ident_f[:], 0.0, [("c", "ident_f")])
        S.add("pool", lambda e: e.affine_select(out=ident_f[:], in_=ident_f[:], pattern=[[-1, 128]],
                                                compare_op=ALU.not_equal, fill=1.0, base=0,
                                                channel_multiplier=1),
              [("c", "ident_f")], [("c", "ident_f")])
        cp("dve", ident_b[:], ident_f[:], [("c", "ident_f")], [("c", "ident_b")])
        memset("pool", Vt[:, :, :, 64:65], 1.0, [("Vones",)])
        iot = sb("iot", [128, 1], I32)
        S.add("pool", lambda e: e.iota(iot[64:96, :], pattern=[[0, 1]], base=0, channel_multiplier=1),
              (), [("c", "iot")])
        MK = [("c", "misc")]
        cp("dve", misc[64:96, 2:3], iot[64:96, :], [("c", "iot")], MK)
        ts("dve", misc[64:96, 3:4], misc[64:96, 2:3], 16.0, None, ALU.is_ge, None, MK, MK)
        stt(misc[64:96, 2:3], misc[64:96, 3:4], -16.0, misc[64:96, 2:3], ALU.mult, ALU.add, MK, MK)
        act(misc[64:96, 0:1], misc[64:96, 2:3], AF.Exp, MK, MK, scale=-math.log(10000.0) / 16.0)
        ts("dve", misc[64:96, 1:2], misc[64:96, 3:4], 2.0, -1.0, ALU.mult, ALU.add, MK, MK)

        ring_state = dict(pos=0)

        def wload(l, name, k0, nk, n0, nw):
            nel = nk * nw
            nu = (nel + 511) // 512
            pos = ring_state["pos"]
            if pos + nu > RING_UNITS:
                pos = 0
            ring_state["pos"] = pos + nu
            W = wb[l][name]
            src = W[k0 * 128:(k0 + nk) * 128, n0:n0 + nw].rearrange("(k p) n -> p k n", p=128)
            dst = ring[:, pos * 512:pos * 512 + nel].rearrange("p (k n) -> p k n", k=nk)
            keys = [("ring", u) for u in range(pos, pos + nu)]
            dma("sp", dst, src, [("wd", wkey(l, name, wgroup(name, k0, n0, nw)))], keys, ("ring", pos))
            return dst, keys

        P_PRE, P_POST, P_FPRE, P_FPOST, P_GATEB, P_QN, P_KVN, P_SCW, P_CFB, P_CFG, P_CFBB = (
            0, 8, 16, 24, 32, 64, 66, 68, 80, 84, 88)

        par_calls = dict(n=0)

        def load_params(l):
            S.label = 'params'
            st1, st2 = pstage[:, 0, :], pstage[:, 1, :]
            rk = [("pstage",)]
            memset("dve", pstage[:], 0.0, rk)
            q = "sp"
            par_calls["n"] += 1
            key = ("par", par_calls["n"])
            dma(q, st1[0:124, :], p_cf_conv_w[l].rearrange("k (c p) -> (k c) p", p=128), (), rk, key, True)

            def ld(off, n, src):
                dma(q, st2[off:off + n, :], src, (), rk, key, True)

            ld(P_PRE, 8, p_norm_mix_pre[l].rearrange("(c p) -> c p", p=128))
            ld(P_POST, 8, p_norm_mix_post[l].rearrange("(c p) -> c p", p=128))
            ld(P_FPRE, 8, p_norm_ffn_pre[l].rearrange("(c p) -> c p", p=128))
            ld(P_FPOST, 8, p_norm_ffn_post[l].rearrange("(c p) -> c p", p=128))
            ld(P_GATEB, 32, p_gate_b[l].rearrange("b (c p) -> (b c) p", p=128))
            ld(P_QN, 2, p_q_norm[l].rearrange("(c p) -> c p", p=128))
            ld(P_KVN, 2, p_kv_norm[l].rearrange("(c p) -> c p", p=128))
            ld(P_SCW, 12, p_sc_conv_w[l].rearrange("k (c p) -> (k c) p", p=128))
            ld(P_CFB, 4, p_cf_conv_b[l].rearrange("(c p) -> c p", p=128))
            ld(P_CFG, 4, p_cf_ln_g[l].rearrange("(c p) -> c p", p=128))
            ld(P_CFBB, 4, p_cf_ln_b[l].rearrange("(c p) -> c p", p=128))
            dma(q, glnB[:, 0, :], p_gm_ln_g[l].partition_broadcast(128), (), [("glnB",)], key, True)
            dma(q, glnB[:, 1, :], p_gm_ln_b[l].partition_broadcast(128), (), [("glnB",)], key, True)
            wsl, wk = arA.view(0, F32, [4, 128])
            dma(q, wsl, p_gm_ws[l].rearrange("g t s -> t g s"), (), wk, key, True)
            b0, b1 = next_bank(), next_bank()
            tr(psb[b0][:, 0:128], st1, ident_f[:], rk + [("c", "ident_f")], [PK(b0)])
            tr(psb[b0][:, 128:256], st2, ident_f[:], rk + [("c", "ident_f")], [PK(b0)])
            cp("dve", par1[:], psb[b0][:, 0:128], [PK(b0)], [("par1",)])
            cp("dve", par2[:], psb[b0][:, 128:256], [PK(b0)], [("par2",)])
            for g in range(4):
                tr(psb[b1][:, g * 128:(g + 1) * 128], wsl[:, g, :], ident_f[:], wk + [("c", "ident_f")], [PK(b1)])
            cp("dve", wsT[:].rearrange("p g t -> p (g t)"), psb[b1][:], [PK(b1)], [("wsT",)])
            ts("dve", cb2[:], par2[:, P_CFB:P_CFB + 4], 2.0, None, ALU.mult, None, [("par2",)], [("cb2",)])
            ts("dve", half_gb[:], par2[:, P_CFG:P_CFG + 8], 0.5, None, ALU.mult, None, [("par2",)], [("half_gb",)])

        PAR = [("par1",), ("par2",)]

        def rstd_from_bank(b, n_feat, eps, out_ap, out_keys, lnv, lnv_keys, extra_reads=()):
            act(lnv, psb[b][:], AF.Ln, [PK(b)] + list(extra_reads), lnv_keys, bias=eps_col(eps), scale=1.0 / n_feat)
            act(out_ap, lnv, AF.Exp, lnv_keys, out_keys, scale=-0.5)

        eps_cols = {}

        def eps_col(v):
            if v not in eps_cols:
                i = 4 + len(eps_cols)
                memset("dve", misc[:, i:i + 1], v, [("c", "eps%d" % i)])
                eps_cols[v] = (misc[:, i:i + 1], ("c", "eps%d" % i))
            return eps_cols[v][0]

        for v in (EPS, 4 * EPS):
            eps_col(v)
        EPSK = [k for (_, k) in eps_cols.values()]

        def load_x(s):
            S.label = 'loadx'
            for tb in range(SEQ // 128):
                st, sk = arA.view((tb % 4) * 4096, F32, [1024])
                dma("sp", st, x_d[s, tb * 128:(tb + 1) * 128, :], (), sk + ([("xload", s)] if tb >= 12 else []), ("xst", tb % 4))
                for half in range(2):
                    b = next_bank()
                    for cc in range(4):
                        c = half * 4 + cc
                        tr(psb[b][:, cc * 128:(cc + 1) * 128], st[:, c * 128:(c + 1) * 128], ident_f[:],
                           sk + [("c", "ident_f")], [PK(b)])
                    cp(evac_eng(), xT[:, half * 4:half * 4 + 4, tb * 128:(tb + 1) * 128],
                       psb[b][:].rearrange("p (c t) -> p c t", c=4), [PK(b)], [("xT", tb // 4, half * 4 + cc) for cc in range(4)])

        out_keys_all = []

        def store_x(s):
            S.label = 'storex'
            for tb in range(SEQ // 128):
                st, sk = arA.view((tb % 4) * 4096, F32, [1024])
                for half in range(2):
                    b = next_bank()
                    for cc in range(4):
                        c = half * 4 + cc
                        tr(psb[b][:, cc * 128:(cc + 1) * 128], xT[:, c, tb * 128:(tb + 1) * 128], ident_f[:],
                           [("xT", tb // 4, c), ("c", "ident_f")], [PK(b)])
                    cp(evac_eng(), st[:, half * 512:(half + 1) * 512], psb[b][:], [PK(b)], sk)
                key = ("ost", tb % 4)
                dma("pool", out_d[s, tb * 128:(tb + 1) * 128, :], st, sk, [("out", s, tb)], key)
                out_keys_all.append(("out", s, tb))

        def rope_tables(s):
            S.label = 'rope'
            TWO_PI = 2.0 * math.pi
            C1 = 6.28125
            C2 = TWO_PI - C1
            R = slice(64, 96)
            for j in range(NTILE):
                posi, k0 = arB.view(0, I32, [NT])
                posf, k1 = arB.view(2048, F32, [NT])
                ang, k2 = arB.view(4096, F32, [2, NT])
                ki, k3 = arB.view(8192, I32, [2, NT])
                kf, k4 = arB.view(12288, F32, [2, NT])
                r1, k5 = arB.view(16384, F32, [2, NT])
                tab, k6 = arB.view(20480, F32, [2, NT])
                dma("pool", posi[R, :], pos_d[s, j * NT:(j + 1) * NT].partition_broadcast(32), (), k0, ("posld",))
                cp("dve", posf[R, :], posi[R, :], k0, k1)
                ts("dve", ang[R, 1, :], posf[R, :], misc[R, 0:1], None, ALU.mult, None, k1 + [("c", "misc")], k2)
                ts("dve", ang[R, 0, :], posf[R, :], misc[R, 0:1], math.pi / 2, ALU.mult, ALU.add, k1 + [("c", "misc")], k2)
                ts("dve", kf[R], ang[R], 1.0 / TWO_PI, None, ALU.mult, None, k2, k4)
                cp("dve", ki[R], kf[R], k4, k3)
                cp("dve", kf[R], ki[R], k3, k4)
                stt(r1[R], kf[R], -C1, ang[R], ALU.mult, ALU.add, k4 + k2, k5)
                stt(r1[R], kf[R], -C2, r1[R], ALU.mult, ALU.add, k4 + k5, k5)
                ts("dve", kf[R], r1[R], math.pi, None, ALU.is_gt, None, k5, k4)
                stt(r1[R], kf[R], -TWO_PI, r1[R], ALU.mult, ALU.add, k4 + k5, k5)
                ts("dve", kf[R], r1[R], -math.pi, None, ALU.is_lt, None, k5, k4)
                stt(r1[R], kf[R], TWO_PI, r1[R], ALU.mult, ALU.add, k4 + k5, k5)
                ts("dve", r1[R], r1[R], math.pi, -math.pi, ALU.min, ALU.max, k5, k5)
                act(tab[R, 0, :], r1[R, 0, :], AF.Sin, k5, k6)
                act(tab[R, 1, :], r1[R, 1, :], AF.Sin, k5 + [("c", "misc")], k6, scale=misc[R, 1:2])
                dma("pool", rope_d[s, j].rearrange("r (a t) -> r a t", a=2), tab[R], k6, [("roped", s, j)], ("ropest",))

        def load_rope(s, j, arena, off):
            rp, rk = arena.view(off, F32, [2, NT])
            dma("act", rp[64:96], rope_d[s, j].rearrange("r (a t) -> r a t", a=2), [("roped", s, j)], rk, ("ropeld", arena.name, off))
            return rp, rk

        def h_from_x(j, gcol0, lo, hi, rstd_ap_fn, rstd_keys):
            t0 = j * NT - HALO
            for c in range(8):
                stt(hT[:, c, lo:hi], xT[:, c, t0 + lo:t0 + hi], par2[:, gcol0 + c:gcol0 + c + 1],
                    rstd_ap_fn(t0 + lo, t0 + hi), ALU.mult, ALU.mult,
                    [("xT", jj, c) for jj in set([(t0 + lo) // NT, (t0 + hi - 1) // NT])] + [("par2",)] + rstd_keys,
                    [("hT", c)])

        def sumsq_rstd(src_ap_fn, nchunks, src_keys, sq, sqk, n_feat, eps, out_ap, out_keys, lnv, lnvk):
            for c in range(nchunks):
                act(sq[:, c, :], src_ap_fn(c), AF.Square, src_keys(c), sqk)
            b = next_bank()
            for c in range(nchunks):
                mm(psb[b][:], ones_b[:], sq[:, c, :], c == 0, c == nchunks - 1, sqk + [("c", "ones_b")], [PK(b)])
            rstd_from_bank(b, n_feat, eps, out_ap, out_keys, lnv, lnvk, EPSK)

        def phase1(l, s):
            S.label = 'p1'
            W1, W1k = wload(l, "win", 0, 8, 256, 288)
            Wkrs, Wkrsk = wload(l, "wkrs", 0, 8, 0, 96)
            Wkv, Wkvk = wload(l, "wukv", 0, 2, 0, 1024)
            sq, sqk = arA.view(0, BF16, [8, NT])
            lnv, lnvk = arA.view(8192, F32, [NT])
            ckv, ckvk = arA.view(10240, F32, [2, NT])
            sqkv, sqkvk = arA.view(14336, BF16, [2, NT])
            rkv, rkvk = arA.view(16384, F32, [NT])
            ckvn, ckvnk = arA.view(18432, BF16, [2, NT])
            tA, tAk = arA.view(20480, F32, [NT])
            tB, tBk = arA.view(22528, F32, [NT])
            h1v, h1k = arB.view(8192, BF16, [8, NT])
            hbufs = [(lambda kc: hT[:, kc, HALO:HALO + NT], [("hT", c) for c in range(8)]),
                     (lambda kc: h1v[:, kc, :], h1k)]
            rps = {}

            def part1(j):
                t0 = j * NT
                for c in range(8):
                    act(sq[:, c, :], xT[:, c, t0:t0 + NT], AF.Square, [("xT", j, c)], sqk)

            def part2(j):
                t0 = j * NT
                hf, hfk = hbufs[j % 2]
                rs1, rs1k = arB.view(4096 + (j % 2) * 2048, F32, [NT])
                rps[j] = load_rope(s, j, arB, 0 if j % 2 == 0 else 16384)
                b = next_bank()
                for c in range(8):
                    mm(psb[b][:], ones_b[:], sq[:, c, :], c == 0, c == 7, sqk + [("c", "ones_b")], [PK(b)])
                rstd_from_bank(b, 1024.0, EPS, rs1, rs1k, lnv, lnvk, EPSK)
                for c in range(8):
                    stt(hf(c), xT[:, c, t0:t0 + NT], par2[:, P_PRE + c:P_PRE + c + 1], rs1, ALU.mult, ALU.mult,
                        [("xT", j, c), ("par2",)] + rs1k, [hfk[c]] if j % 2 == 0 else hfk)
                dma("pool", rstd_d[t0:t0 + NT].rearrange("(o n) -> o n", o=1), rs1[0:1, :], rs1k, [("rstdd", j)],
                    ("rstdst", j % 2))

            part1(0)
            part2(0)
            for j in range(NTILE):
                t0 = j * NT
                hm, hk = hbufs[j % 2]
                rp, rpk = rps[j]
                if j + 1 < NTILE:
                    part1(j + 1)
                for cc in range(2):
                    b = next_bank()
                    for kc in range(8):
                        mm(psb[b][:], W1[:, kc, cc * 128:(cc + 1) * 128], hm(kc), kc == 0, kc == 7, hk + W1k, [PK(b)])
                    cp("dve", ckv[:, cc, :], psb[b][:], [PK(b)], ckvk)
                    act(sqkv[:, cc, :], psb[b][:], AF.Square, [PK(b)], sqkvk)
                ba, bb = next_bank(), next_bank()
                bank_state["held"].update((ba, bb))
                for kc in range(8):
                    mm(psb[ba][0:96, :], W1[:, kc, 192:288], hm(kc), kc == 0, kc == 7, hk + W1k, [PK(ba)])
                for kc in range(8):
                    mm(psb[bb][0:96, :], Wkrs[:, kc, 0:96], hm(kc), kc == 0, kc == 7, hk + Wkrsk, [PK(bb)])
                if j + 1 < NTILE:
                    part2(j + 1)
                b = next_bank()
                for cc in range(2):
                    mm(psb[b][:], ones_b[:], sqkv[:, cc, :], cc == 0, cc == 1, sqkvk + [("c", "ones_b")], [PK(b)])
                rstd_from_bank(b, 256.0, EPS, rkv, rkvk, lnv, lnvk, EPSK)
                for cc in range(2):
                    stt(ckvn[:, cc, :], ckv[:, cc, :], par2[:, P_KVN + cc:P_KVN + cc + 1], rkv, ALU.mult, ALU.mult,
                        ckvk + rkvk + [("par2",)], ckvnk)
                R = slice(64, 96)
                tt("dve", tA[R], psb[ba][R, :], rp[R, 0, :], ALU.mult, [PK(ba)] + rpk, tAk)
                tt("dve", tB[R], psb[bb][R, :], rp[R, 1, :], ALU.mult, [PK(bb)] + rpk, tBk)
                bank_state["held"].difference_update((ba, bb))
                tt("pool", KT[R, 0, t0:t0 + NT], tA[R], tB[R], ALU.add, tAk + tBk, [("KTr", j, 0)])
                for h in range(1, HEADS):
                    cp("pool", KT[R, h, t0:t0 + NT], KT[R, 0, t0:t0 + NT], [("KTr", j, 0)], [("KTr", j, h)])
                for h in range(HEADS):
                    b = next_bank()
                    for kc in range(2):
                        mm(psb[b][0:64, :], Wkv[:, kc, h * 128:h * 128 + 64], ckvn[:, kc, :], kc == 0, kc == 1,
                           ckvnk + Wkvk, [PK(b)])
                    cp(evac_eng(), KT[0:64, h, t0:t0 + NT], psb[b][0:64, :], [PK(b)], [("KT", j, h)])
                Wv = Wkv.rearrange("p k (h d) -> p k h d", h=HEADS)
                for tb in range(4):
                    b = next_bank()
                    for kc in range(2):
                        mm(psb[b][:].rearrange("p (h d) -> p h d", h=HEADS), ckvn[:, kc, tb * 128:(tb + 1) * 128],
                           Wv[:, kc, :, 64:128], kc == 0, kc == 1, ckvnk + Wkvk, [PK(b)])
                    cp(evac_eng(), Vt[:, j * 4 + tb, :, 0:64], psb[b][:].rearrange("p (h d) -> p h d", h=HEADS),
                       [PK(b)], [("Vt", j * 4 + tb)])

        SIG_OFFS = [20480, 22528, 0, 2048, 4096]

        def branch_out(l, j, bi, wname, actT, actk, merged, mk):
            S.label = 'p2.bo%d' % bi
            hk = [("hT", c) for c in range(8)]
            NS = len(SIG_OFFS)
            LEAD = NS - 1
            loads = {}
            sigs = {}

            def get_w(pr):
                if pr not in loads:
                    loads[pr] = (wload(l, wname, 0, 4, pr * 256, 256),
                                 wload(l, "win", 0, 8, C_GATE + bi * 1024 + pr * 256, 256))
                return loads[pr]

            def gate(c):
                _, (Wg, Wgk) = get_w(c // 2)
                cc = c % 2
                bg = next_bank()
                for kc in range(8):
                    mm(psb[bg][:], Wg[:, kc, cc * 128:(cc + 1) * 128], hT[:, kc, HALO:HALO + NT], kc == 0, kc == 7,
                       hk + Wgk, [PK(bg)])
                sig, sigk = arB.view(SIG_OFFS[c % NS], F32, [NT])
                hb = half_gateb[:, bi * 8 + c:bi * 8 + c + 1]
                act(sig, psb[bg][:], AF.Tanh, [PK(bg), ("half_gateb",)], sigk, bias=hb, scale=0.5)
                sigs[c] = (sig, sigk)

            def ymerge(c):
                (Wo, Wok), _ = get_w(c // 2)
                cc = c % 2
                by = next_bank()
                for kc in range(4):
                    mm(psb[by][:], Wo[:, kc, cc * 128:(cc + 1) * 128], actT[:, kc, :], kc == 0, kc == 3,
                       actk + Wok, [PK(by)])
                sig, sigk = sigs.pop(c)
                gt, gtk = arA.view(20480 + (c % 2) * 2048, F32, [NT])
                mslice = merged[:, c, :]
                if bi == 0:
                    stt(mslice, sig, 1.0, psb[by][:], ALU.add, ALU.mult, sigk + [PK(by)], [mk[c]])
                else:
                    stt(gt, sig, 1.0, psb[by][:], ALU.add, ALU.mult, sigk + [PK(by)], gtk)
                    tt("dve", mslice, mslice, gt, ALU.add, gtk + [mk[c]], [mk[c]])

            for c in range(LEAD):
                gate(c)
            for c in range(8):
                if c + LEAD < 8:
                    gate(c + LEAD)
                ymerge(c)

        half_gateb = sb("half_gateb", [128, 32], F32)

        diag_state = dict(i=0)

        def dwconv(z, zk, ntap, wcol_fn, bank_for_chunk):
            for c in range(4):
                b = bank_for_chunk(c)
                for k in range(ntap):
                    di = diag_state["i"]
                    diag_state["i"] = (di + 1) % 6
                    dg = diag[di]
                    ts("dve", dg[:], ident_b[:], wcol_fn(k, c), None, ALU.mult, None,
                       [("c", "ident_b")] + PAR, [("diag", di)])
                    mm(psb[b][:], dg[:], z[:, c, k:k + NT], k == 0, k == ntap - 1, zk + [("diag", di)], [PK(b)])

        def prep_h(j):
            lab_save = S.label
            S.label = 'p2.h'
            t0 = j * NT
            hk = [("hT", c) for c in range(8)]
            rwm, rwmk = arA.view(22528, F32, [NT])
            dma("act", rwm, rstd_d[t0:t0 + NT].partition_broadcast(128), [("rstdd", j)], rwmk, ("rstdld", "m"))
            for c in range(8):
                stt(hT[:, c, HALO:HALO + NT], xT[:, c, t0:t0 + NT], par2[:, P_PRE + c:P_PRE + c + 1], rwm,
                    ALU.mult, ALU.mult, [("xT", j, c), ("par2",)] + rwmk, [("hT", c)])
            if j < NTILE - 1:
                dma("act", rwh[:, 0:HALO], rstd_d[t0 + NT:t0 + NT + HALO].partition_broadcast(128), [("rstdd", j + 1)],
                    [("rwh",)], ("rstdld", "h"))
                for c in range(8):
                    stt(hT[:, c, HALO + NT:HW], xT[:, c, t0 + NT:t0 + NT + HALO], par2[:, P_PRE + c:P_PRE + c + 1],
                        rwh[:, 0:HALO], ALU.mult, ALU.mult, [("xT", j + 1, c), ("par2",), ("rwh",)], [("hT", c)])
            else:
                memset("pool", hT[:, :, HALO + NT:HW], 0.0, hk)
            S.label = lab_save

        def phase2(l, s, j):
            t0 = j * NT
            S.label = 'p2.h'
            hk = [("hT", c) for c in range(8)]
            if j == 0:
                prep_h(0)
            hm = lambda kc: hT[:, kc, HALO:HALO + NT]

            S.label = 'p2.q'
            cq, cqk = arB.view(0, F32, [2, NT])
            sqq, sqqk = arB.view(4096, BF16, [2, NT])
            rq, rqk = arB.view(6144, F32, [NT])
            cqn, cqnk = arB.view(8192, BF16, [2, NT])
            oT, oTk = arB.view(10240, BF16, [4, NT])
            rec, reck = arB.view(14336, F32, [4])
            qT, qTk = arA.view(0, BF16, [8, NT])
            on, onk = arA.view(8192, BF16, [4, NT])
            rp, rpk = load_rope(s, j, arA, 14336)
            tA, tAk = arA.view(18432, F32, [NT])
            tB, tBk = arA.view(20480, F32, [NT])
            lnv, lnvk = arA.view(22528, F32, [NT])
            Wq0, Wq0k = wload(l, "win", 0, 8, C_Q, 256)
            for cc in range(2):
                b = next_bank()
                for kc in range(8):
                    mm(psb[b][:], Wq0[:, kc, cc * 128:(cc + 1) * 128], hm(kc), kc == 0, kc == 7, hk + Wq0k, [PK(b)])
                cp("dve", cq[:, cc, :], psb[b][:], [PK(b)], cqk)
                act(sqq[:, cc, :], psb[b][:], AF.Square, [PK(b)], sqqk)
            b = next_bank()
            for cc in range(2):
                mm(psb[b][:], ones_b[:], sqq[:, cc, :], cc == 0, cc == 1, sqqk + [("c", "ones_b")], [PK(b)])
            rstd_from_bank(b, 256.0, EPS, rq, rqk, lnv, lnvk, EPSK)
            for cc in range(2):
                stt(cqn[:, cc, :], cq[:, cc, :], par2[:, P_QN + cc:P_QN + cc + 1], rq, ALU.mult, ALU.mult,
                    cqk + rqk + [("par2",)], cqnk)
            Wuq, Wuqk = wload(l, "wuq", 0, 2, 0, 768)
            Wuqs, Wuqsk = wload(l, "wuqs", 0, 2, 0, 768)
            R = slice(64, 96)
            for h in range(HEADS):
                ba, bb = next_bank(), next_bank()
                for kc in range(2):
                    mm(psb[ba][0:96, :], Wuq[:, kc, h * 96:(h + 1) * 96], cqn[:, kc, :], kc == 0, kc == 1,
                       cqnk + Wuqk, [PK(ba)])
                for kc in range(2):
                    mm(psb[bb][0:96, :], Wuqs[:, kc, h * 96:(h + 1) * 96], cqn[:, kc, :], kc == 0, kc == 1,
                       cqnk + Wuqsk, [PK(bb)])
                cp("act", qT[0:64, h, :], psb[ba][0:64, :], [PK(ba)], [("A", h // 2)])
                tt("dve", tA[R], psb[ba][R, :], rp[R, 0, :], ALU.mult, [PK(ba)] + rpk, tAk)
                tt("dve", tB[R], psb[bb][R, :], rp[R, 1, :], ALU.mult, [PK(bb)] + rpk, tBk)
                tt("dve", qT[R, h, :], tA[R], tB[R], ALU.add, tAk + tBk, [("A", h // 2)])

            S.label = 'p2.attn'
            scale = 96.0 ** -0.5
            NKT = SEQ // 128
            seqn = [(h, kt) for h in range(HEADS) for kt in range(NKT)]
            sbank = {}
            obank = {}

            def emit_S(i):
                h, kt = seqn[i]
                b_ = next_bank()
                sbank[i] = b_
                mm(psb[b_][:], KT[0:96, h, kt * 128:(kt + 1) * 128], qT[0:96, h, :], True, True,
                   [("KT", kt // 4, h), ("KTr", kt // 4, h), ("A", h // 2)], [PK(b_)])

            LOOK = 2
            for i in range(min(LOOK, len(seqn))):
                emit_S(i)
            for i, (h, kt) in enumerate(seqn):
                if kt == 0:
                    bo = next_bank()
                    bank_state["held"].add(bo)
                    obank[h] = bo
                bo = obank[h]
                Ob = psb[bo][:, 0:260].rearrange("p (q d) -> p q d", q=4)
                bs_ = sbank.pop(i)
                pT = pTb[:, i % 3, :]
                pTk = [("pT", i % 3)]
                act(pT, psb[bs_][:], AF.Exp, [PK(bs_)], pTk, scale=scale)
                if i + LOOK < len(seqn):
                    emit_S(i + LOOK)
                for qb in range(4):
                    mm(Ob[:, qb, :], pT[:, qb * 128:(qb + 1) * 128], Vt[:, kt, h, :],
                       (kt == 0 and qb == 0), (kt == NKT - 1),
                       pTk + [("Vt", kt), ("Vones",)], [PK(bo)], skip_group_check=True)
                if kt == NKT - 1:
                    S.add("dve", lambda e, Ob=Ob, rec=rec: e.reciprocal(out=rec, in_=Ob[:, :, 64]),
                          [PK(bo)], reck)
                    for qb in range(4):
                        ts("dve", on[:, qb, h * 64:(h + 1) * 64], Ob[:, qb, 0:64], rec[:, qb:qb + 1], None, ALU.mult, None,
                           [PK(bo)] + reck, [("A", 4 + qb // 2)])
                    bank_state["held"].discard(bo)
            S.label = 'p2.attnT'
            for c in range(4):
                b = next_bank()
                pb = psb[b][:].bitcast(BF16)
                for qb in range(4):
                    tr(pb[:, qb * 128:(qb + 1) * 128], on[:, qb, c * 128:(c + 1) * 128], ident_b[:],
                       [("A", 4 + qb // 2), ("c", "ident_b")], [PK(b)])
                cp(evac_eng(), oT[:, c, :], pb[:, 0:NT], [PK(b)], oTk)

            merged, _mk = arA.view(0, F32, [8, NT])
            mk = [("A", c) for c in range(8)]
            mb, mbk = arA.view(16384, BF16, [8, NT])
            branch_out(l, j, 0, "womla", oT, oTk, merged, mk)

            S.label = 'p2.sc'
            scb, scbk = arB.view(0, BF16, [4, NT])
            zsc, zsck = arB.view(4096, BF16, [4, 516])
            asc, asck = arB.view(10240, BF16, [4, NT])
            ctmp, ctmpk = arB.view(14336, F32, [NT])
            for half in range(2):
                Wb, Wbk = wload(l, "win", 0, 8, C_SCB + half * 256, 256)
                for cc in range(2):
                    c = half * 2 + cc
                    b = next_bank()
                    for kc in range(8):
                        mm(psb[b][:], Wb[:, kc, cc * 128:(cc + 1) * 128], hm(kc), kc == 0, kc == 7, hk + Wbk, [PK(b)])
                    cp(evac_eng(), scb[:, c, :], psb[b][:], [PK(b)], scbk)
            for half in range(2):
                Wc, Wck = wload(l, "win", 0, 8, C_SCC + half * 256, 256)
                Wx, Wxk = wload(l, "win", 0, 8, C_SCX + half * 256, 256)
                for cc in range(2):
                    c = half * 2 + cc
                    wc = lambda kc: Wc[:, kc, cc * 128:(cc + 1) * 128]
                    wx = lambda kc: Wx[:, kc, cc * 128:(cc + 1) * 128]
                    bc, bx = next_bank(), next_bank()
                    for kc in range(8):
                        mm(psb[bc][:], wc(kc), hm(kc), kc == 0, kc == 7, hk + Wck, [PK(bc)])
                    for kc in range(8):
                        mm(psb[bx][:], wx(kc), hm(kc), kc == 0, kc == 7, hk + Wxk, [PK(bx)])
                    cp("act", ctmp, psb[bc][:], [PK(bc)], ctmpk)
                    tt("dve", zsc[:, c, 1:1 + NT], ctmp, psb[bx][:], ALU.mult, ctmpk + [PK(bx)], zsck)
                    bc2, bx2 = next_bank(), next_bank()
                    col = HALO + NT
                    for kc in range(8):
                        mm(psb[bc2][:, 0:2], wc(kc), hT[:, kc, col:col + 2], kc == 0, kc == 7, hk + Wck, [PK(bc2)])
                    for kc in range(8):
                        mm(psb[bx2][:, 0:2], wx(kc), hT[:, kc, col:col + 2], kc == 0, kc == 7, hk + Wxk, [PK(bx2)])
                    cp("act", ctmp[:, 0:2], psb[bc2][:, 0:2], [PK(bc2)], ctmpk)
                    tt("dve", zsc[:, c, 1 + NT:2 + NT], ctmp[:, 0:1], psb[bx2][:, 0:1], ALU.mult, ctmpk + [PK(bx2)], zsck)
            if j == 0:
                memset("pool", zsc[:, :, 0:1], 0.0, zsck)
            else:
                cp("pool", zsc[:, :, 0:1], zprev_sc[:, :, 0:1], [("zprev_sc",)], zsck)
            cp("pool", zprev_sc[:, :, 0:1], zsc[:, :, NT:NT + 1], zsck, [("zprev_sc",)])
            sc_banks = [next_bank() for _ in range(4)]
            dwconv(zsc, zsck, 3, lambda k, c: par2[:, P_SCW + k * 4 + c:P_SCW + k * 4 + c + 1], lambda c: sc_banks[c])
            for c in range(4):
                tt("dve", asc[:, c, :], scb[:, c, :], psb[sc_banks[c]][:], ALU.mult, scbk + [PK(sc_banks[c])], asck)
            branch_out(l, j, 1, "wosc", asc, asck, merged, mk)

            S.label = 'p2.cf'
            zcf, zcfk = arB.view(0, BF16, [4, 544])
            ycf, ycfk = arB.view(6144, F32, [4, NT])
            ybf, ybfk = arB.view(14336, BF16, [4, NT])
            mean, meank = arB.view(18432, F32, [NT])
            ysq, ysqk = arB.view(20480, BF16, [4, NT])
            lnv2, lnv2k = arA.view(16384, F32, [NT])
            rcf, rcfk = arA.view(18432, F32, [NT])
            for half in range(2):
                Wa, Wak = wload(l, "win", 0, 8, C_CFA + half * 256, 256)
                Wg_, Wgk_ = wload(l, "win", 0, 8, C_CFG + half * 256, 256)
                for cc in range(2):
                    c = half * 2 + cc
                    wa = lambda kc: Wa[:, kc, cc * 128:(cc + 1) * 128]
                    wg = lambda kc: Wg_[:, kc, cc * 128:(cc + 1) * 128]
                    ba, bg = next_bank(), next_bank()
                    for kc in range(8):
                        mm(psb[ba][:], wa(kc), hm(kc), kc == 0, kc == 7, hk + Wak, [PK(ba)])
                    for kc in range(8):
                        mm(psb[bg][:], wg(kc), hm(kc), kc == 0, kc == 7, hk + Wgk_, [PK(bg)])
                    act(mean, psb[bg][:], AF.Tanh, [PK(bg)], meank, scale=0.5)
                    stt(zcf[:, c, HALO:HALO + NT], mean, 1.0, psb[ba][:], ALU.add, ALU.mult, meank + [PK(ba)], zcfk)
                    ba2, bg2 = next_bank(), next_bank()
                    col = HALO + NT
                    for kc in range(8):
                        mm(psb[ba2][:, 0:HALO], wa(kc), hT[:, kc, col:col + HALO], kc == 0, kc == 7, hk + Wak, [PK(ba2)])
                    for kc in range(8):
                        mm(psb[bg2][:, 0:HALO], wg(kc), hT[:, kc, col:col + HALO], kc == 0, kc == 7, hk + Wgk_, [PK(bg2)])
                    act(mean[:, 0:16], psb[bg2][:, 0:16], AF.Tanh, [PK(bg2)], meank, scale=0.5)
                    stt(zcf[:, c, HALO + NT:HW], mean[:, 0:HALO], 1.0, psb[ba2][:, 0:HALO], ALU.add, ALU.mult,
                        meank + [PK(ba2)], zcfk)
            if j == 0:
                memset("pool", zcf[:, :, 0:HALO], 0.0, zcfk)
            else:
                cp("pool", zcf[:, :, 0:HALO], zprev_cf[:, :, 0:HALO], [("zprev_cf",)], zcfk)
            cp("pool", zprev_cf[:, :, 0:HALO], zcf[:, :, NT:NT + HALO], zcfk, [("zprev_cf",)])
            cf_banks = [next_bank() for _ in range(4)]
            dwconv(zcf, zcfk, 31, lambda k, c: par1[:, k * 4 + c:k * 4 + c + 1], lambda c: cf_banks[c])
            for c in range(4):
                act(ycf[:, c, :], psb[cf_banks[c]][:], AF.Identity, [PK(cf_banks[c]), ("cb2",)], ycfk, bias=cb2[:, c:c + 1])
                cp("dve", ybf[:, c, :], ycf[:, c, :], ycfk, ybfk)
                act(ysq[:, c, :], ycf[:, c, :], AF.Square, ycfk, ysqk)
            b1, b2 = next_bank(), next_bank()
            for c in range(4):
                mm(psb[b1][:], ones_b[:], ybf[:, c, :], c == 0, c == 3, ybfk + [("c", "ones_b")], [PK(b1)])
            for c in range(4):
                mm(psb[b2][:], ones_b[:], ysq[:, c, :], c == 0, c == 3, ysqk + [("c", "ones_b")], [PK(b2)])
            ts("dve", mean, psb[b1][:], 1.0 / 512.0, None, ALU.mult, None, [PK(b1)], meank)
            msq, msqk = arB.view(0, F32, [NT])
            tt("dve", msq, mean, mean, ALU.mult, meank, msqk)
            stt(lnv2, psb[b2][:], 1.0 / 512.0, msq, ALU.mult, ALU.subtract, [PK(b2)] + msqk, lnv2k)
            act(lnv2, lnv2, AF.Ln, lnv2k + EPSK, lnv2k, bias=eps_col(4 * EPS))
            act(rcf, lnv2, AF.Exp, lnv2k, rcfk, scale=-0.5)
            acf = ybf
            acfk = ybfk
            for c in range(4):
                tt("dve", ycf[:, c, :], ycf[:, c, :], mean, ALU.subtract, ycfk + meank, ycfk)
                tt("dve", ycf[:, c, :], ycf[:, c, :], rcf, ALU.mult, ycfk + rcfk, ycfk)
                ts("dve", ycf[:, c, :], ycf[:, c, :], half_gb[:, c:c + 1], half_gb[:, 4 + c:5 + c], ALU.mult, ALU.add,
                   ycfk + [("half_gb",)], ycfk)
                act(lnv2, ycf[:, c, :], AF.Tanh, ycfk, lnv2k)
                stt(acf[:, c, :], lnv2, 1.0, ycf[:, c, :], ALU.add, ALU.mult, lnv2k + ycfk, acfk)
            branch_out(l, j, 2, "wocf", acf, acfk, merged, mk)

            S.label = 'p2.gm'
            gu, guk = arB.view(0, BF16, [4, NT])
            vn, vnk = arB.view(8192, BF16, [4, NT])
            stt_, sttk = arB.view(12288, F32, [64])
            agm, agmk = arB.view(14336, BF16, [4, NT])
            bsrow, bsrowk = arB.view(18432, F32, [NT])
            dma("act", bsrow[0:1, :], p_gm_bs[l].rearrange("g t -> (g t)").rearrange("(o n) -> o n", o=1), (),
                bsrowk, ("bsld",))
            for half in range(2):
                Wu, Wuk = wload(l, "win", 0, 8, C_GMU + half * 256, 256)
                for cc in range(2):
                    c = half * 2 + cc
                    b = next_bank()
                    for kc in range(8):
                        mm(psb[b][:], Wu[:, kc, cc * 128:(cc + 1) * 128], hm(kc), kc == 0, kc == 7, hk + Wuk, [PK(b)])
                    act(gu[:, c, :], psb[b][:], AF.Gelu_apprx_tanh, [PK(b)], guk)
            Wv_, Wvk_ = wload(l, "win", 0, 8, C_GMV, 512)
            v4 = [arA.view(16384 + i * 2048, F32, [NT]) for i in range(4)]
            for tb in range(4):
                b = next_bank()
                for kc in range(8):
                    mm(psb[b][:], hT[:, kc, HALO + tb * 128:HALO + (tb + 1) * 128], Wv_[:, kc, :], kc == 0, kc == 7,
                       hk + Wvk_, [PK(b)])
                v, vk = v4[tb]
                act(v, psb[b][:], AF.Gelu_apprx_tanh, [PK(b)], vk)
                st6 = stt_[:, 32 + tb * 8:32 + tb * 8 + 6]
                mv = stt_[:, tb * 2:tb * 2 + 2]
                S.add("dve", lambda e, st6=st6, v=v: e.bn_stats(out=st6, in_=v), vk, sttk)
                S.add("dve", lambda e, st6=st6, mv=mv: e.bn_aggr(out=mv, in_=st6), sttk, sttk)
            mv4 = stt_[:, 0:8].rearrange("p (t two) -> p t two", two=2)
            act(stt_[:, 8:12], mv4[:, :, 1], AF.Ln, sttk + EPSK, sttk, bias=eps_col(EPS))
            act(stt_[:, 12:16], stt_[:, 8:12], AF.Exp, sttk, sttk, scale=-0.5)
            gm_banks = [next_bank() for _ in range(4)]
            for tb in range(4):
                v, vk = v4[tb]
                ts("dve", v, v, stt_[:, tb * 2:tb * 2 + 1], stt_[:, 12 + tb:13 + tb], ALU.subtract, ALU.mult, vk + sttk, vk)
                tt("dve", v, v, glnB[:, 0, :], ALU.mult, vk + [("glnB",)], vk)
                tt("dve", vn[:, tb, :], v, glnB[:, 1, :], ALU.add, vk + [("glnB",)], [("vn", tb)])
                for g in range(4):
                    ob = psb[gm_banks[g]][:, tb * 128:(tb + 1) * 128]
                    mm(ob, vn[:, tb, g * 128:(g + 1) * 128], wsT[:, g, :], True, False, [("vn", tb), ("wsT",)],
                       [PK(gm_banks[g])], skip_group_check=True)
                    mm(ob, ones_f[0:1, :], bsrow[0:1, g * 128:(g + 1) * 128], False, True,
                       bsrowk + [("c", "ones_f")], [PK(gm_banks[g])], skip_group_check=True)
            for g in range(4):
                tt("dve", agm[:, g, :], gu[:, g, :], psb[gm_banks[g]][:], ALU.mult, guk + [PK(gm_banks[g])], agmk)
            branch_out(l, j, 3, "wogm", agm, agmk, merged, mk)

            S.label = 'p2.wout'
            for c in range(8):
                cp("act" if c % 2 else "dve", mb[:, c, :], merged[:, c, :], [mk[c]], mbk)
            m, _ = arB.view(0, F32, [8, NT])
            mkk = [("B", c) for c in range(8)]
            msq8, msq8k = arB.view(16384, BF16, [8, NT])
            lnv3, lnv3k = arA.view(0, F32, [NT])
            rp3, rp3k = arA.view(2048, F32, [NT])

            def lin8(wname, rhs, rhsk, nk_total, dst, dstk, sqdst, sqk_):
                for half in range(2):
                    banks = [next_bank() for _ in range(4)]
                    k0 = 0
                    while k0 < nk_total:
                        nk = min(8, nk_total - k0)
                        Wt, Wtk = wload(l, wname, k0, nk, half * 512, 512)
                        for cc in range(4):
                            for kc in range(nk):
                                mm(psb[banks[cc]][:], Wt[:, kc, cc * 128:(cc + 1) * 128], rhs[:, k0 + kc, :],
                                   (k0 + kc) == 0, (k0 + kc) == nk_total - 1, rhsk + Wtk, [PK(banks[cc])],
                                   skip_group_check=True)
                        k0 += nk
                    for cc in range(4):
                        c = half * 4 + cc
                        cp("dve", dst[:, c, :], psb[banks[cc]][:], [PK(banks[cc])], [dstk[c]])
                        act(sqdst[:, c, :], psb[banks[cc]][:], AF.Square, [PK(banks[cc])], sqk_)

            def post_norm_residual(gcol, eps, src, srck):
                b = next_bank()
                for c in range(8):
                    mm(psb[b][:], ones_b[:], msq8[:, c, :], c == 0, c == 7, msq8k + [("c", "ones_b")], [PK(b)])
                rstd_from_bank(b, 1024.0, eps, rp3, rp3k, lnv3, lnv3k, EPSK)
                for c in range(8):
                    stt(src[:, c, :], src[:, c, :], par2[:, gcol + c:gcol + c + 1], rp3, ALU.mult, ALU.mult,
                        [srck[c], ("par2",)] + rp3k, [srck[c]])
                    tt("dve", xT[:, c, t0:t0 + NT], xT[:, c, t0:t0 + NT], src[:, c, :], ALU.add,
                       [srck[c], ("xT", j, c)], [("xT", j, c)])

            lin8("wout", mb, mbk, 8, m, mkk, msq8, msq8k)
            post_norm_residual(P_POST, 4 * EPS, m, mkk)

            S.label = 'p2.ffn_in'
            sq, sqk = arA.view(0, BF16, [8, NT])
            rf, rfk = arA.view(8192, F32, [NT])
            lnv4, lnv4k = arA.view(10240, F32, [NT])
            sumsq_rstd(lambda c: xT[:, c, t0:t0 + NT], 8, lambda c: [("xT", j, c)], sq, sqk, 1024.0, EPS,
                       rf, rfk, lnv4, lnv4k)
            for c in range(8):
                stt(hT[:, c, HALO:HALO + NT], xT[:, c, t0:t0 + NT], par2[:, P_FPRE + c:P_FPRE + c + 1], rf,
                    ALU.mult, ALU.mult, [("xT", j, c), ("par2",)] + rfk, [("hT", c)])
            f, _ = arA.view(0, BF16, [22, NT])
            fk = [("A", i) for i in range(11)]
            for blk in range(FFN_H // 256):
                n0 = blk * 256
                Wg, Wgk = wload(l, "wfin", 0, 8, n0, 256)
                Wu2, Wu2k = wload(l, "wfin", 0, 8, FFN_H + n0, 256)
                for cc in range(2):
                    hc = blk * 2 + cc
                    bg, bu = next_bank(), next_bank()
                    for kc in range(8):
                        mm(psb[bg][:], Wg[:, kc, cc * 128:(cc + 1) * 128], hm(kc), kc == 0, kc == 7, hk + Wgk, [PK(bg)])
                    for kc in range(8):
                        mm(psb[bu][:], Wu2[:, kc, cc * 128:(cc + 1) * 128], hm(kc), kc == 0, kc == 7, hk + Wu2k, [PK(bu)])
                    th, thk = arB.view(20480 + (hc % 2) * 2048, F32, [NT])
                    aa, aak = arA.view(22528, F32, [NT])
                    act(th, psb[bg][:], AF.Tanh, [PK(bg)], thk, scale=0.5)
                    stt(aa, th, 1.0, psb[bg][:], ALU.add, ALU.mult, thk + [PK(bg)], aak)
                    tt("dve", f[:, hc, :], aa, psb[bu][:], ALU.mult, aak + [PK(bu)], [("A", hc // 2)])
            if j + 1 < NTILE:
                prep_h(j + 1)
            S.label = 'p2.ffn_out'
            m2, _ = arB.view(0, F32, [8, NT])
            lin8("wfout", f, fk, 22, m2, mkk, msq8, msq8k)
            post_norm_residual(P_FPOST, 4 * EPS, m2, mkk)

        import os
        STAGE = int(os.environ.get("K_STAGE", "9"))
        for s in range(NSEQ):
            load_x(s)
            if s == 0 and STAGE >= 1:
                cast_gate.append(("xload", 0))
                emit_casts(layers[0])
                cast_gate.clear()
            if STAGE >= 2:
                rope_tables(s)
            for l in layers:
                if STAGE >= 2:
                    load_params(l)
                    ts("dve", half_gateb[:], par2[:, P_GATEB:P_GATEB + 32], 0.5, None, ALU.mult, None, [("par2",)],
                       [("half_gateb",)])
                if STAGE >= 3:
                    phase1(l, s)
                if s == 0 and l == layers[0] and STAGE >= 1:
                    for l2 in layers[1:]:
                        emit_casts(l2)
                if STAGE >= 4:
                    for j in range(NTILE):
                        phase2(l, s, j)
            store_x(s)
        S.add("sp", lambda e: None, out_keys_all, ())
        S.emit(nc, stack)
    nc._sched = S
    return nc


_PARAM_NAMES = ["norm_mix_pre", "w_in", "mla_q_norm", "w_uq", "mla_kv_norm", "w_ukv", "w_o_mla", "sc_conv_w",
                "w_o_sc", "cf_conv_w", "cf_conv_b", "cf_ln_g", "cf_ln_b", "w_o_cf", "gm_ln_g", "gm_ln_b", "gm_ws",
                "gm_bs", "w_o_gm", "gate_b", "w_out", "norm_mix_post", "norm_ffn_pre", "w_ffn_in", "w_ffn_out",
                "norm_ffn_post"]

FUSED = True


def kernel(**inputs):
    x = np.ascontiguousarray(np.asarray(inputs["x"], dtype=np.float32))
    pos = np.ascontiguousarray(np.asarray(inputs["positions"], dtype=np.int32))
    params = {k: np.ascontiguousarray(np.asarray(inputs[k], dtype=np.float32)) for k in _PARAM_NAMES}
    n = 8
    launches = [[0, 1]] if FUSED else [[0], [1]]
    cur = x
    for layers in launches:
        nc = build_program(layers)
        in_maps = []
        for c in range(n):
            m = {"x": np.ascontiguousarray(cur[c * NSEQ:(c + 1) * NSEQ]),
                 "positions": np.ascontiguousarray(pos[c * NSEQ:(c + 1) * NSEQ])}
            m.update(params)
            in_maps.append(m)
        res = run_bass_kernel_spmd(nc, in_maps, core_ids=list(range(n)))
        cur = np.concatenate([np.asarray(r["out"]) for r in res.results], axis=0).astype(np.float32)
    return cur
```
